# Optimizing a Trainium2 kernel written in Bass

```python
import jax, jax.numpy as jnp
from jax import lax
import numpy as np

D_MODEL = 1024
BATCH = 32
SEQ = 2048
DEPTH = 4

GRID_W = 64
CTX_LEN = 256
D_MIX = D_MODEL
MLA_HEADS = 8
Q_RANK = 256
KV_RANK = 128
QK_NOPE = 64
QK_ROPE = 32
V_HEAD = 64
MLA_SCALE = (QK_NOPE + QK_ROPE) ** -0.5
ROPE_BASE = 10000.0
Q_BLOCK = 128
RWKV_HEADS = 8
RWKV_HEAD = 64
RWKV_DIM = RWKV_HEADS * RWKV_HEAD
DECAY_LORA = 64
AAA_LORA = 64
MV_LORA = 32
GATE_LORA = 128
N_DIRS = 2
LNX_EPS = 64e-5
D_FF = 4 * D_MODEL
NORM_EPS = 1e-6

MLA_IN = Q_RANK + KV_RANK + QK_ROPE
RWKV_IN = 3 * RWKV_DIM + N_DIRS * DECAY_LORA + N_DIRS * AAA_LORA + GATE_LORA
C_IN = MLA_IN + RWKV_IN
RWKV_SPLITS = (RWKV_DIM, 2 * RWKV_DIM, 3 * RWKV_DIM,
               3 * RWKV_DIM + N_DIRS * DECAY_LORA,
               3 * RWKV_DIM + N_DIRS * DECAY_LORA + N_DIRS * AAA_LORA)

kernel_name = 'hybrid_mla_rwkv7_diffusion_trunk'


def rmsnorm(x, g):
    xf = x.astype(jnp.float32)
    y = xf * lax.rsqrt(jnp.mean(xf * xf, axis=-1, keepdims=True) + NORM_EPS)
    return (y * g.astype(jnp.float32)).astype(x.dtype)


def modulate(h, shift, scale):
    return h * (1 + scale[:, None]) + shift[:, None]


def sq_relu_mlp(h, w1, w2):
    return jnp.square(jax.nn.relu(h @ w1)) @ w2


def axial_angles(n_tokens):
    rows = n_tokens // GRID_W
    row = jnp.repeat(jnp.arange(rows, dtype=jnp.float32), GRID_W)
    col = jnp.tile(jnp.arange(GRID_W, dtype=jnp.float32), rows)
    n_freq = QK_ROPE // 4
    inv_freq = ROPE_BASE ** (-jnp.arange(n_freq, dtype=jnp.float32) / n_freq)
    ar = row[:, None] * inv_freq
    ac = col[:, None] * inv_freq
    return jnp.concatenate([ar, ar, ac, ac], axis=-1)


def apply_axial_rope(x, angles):
    shape = (angles.shape[0],) + (1,) * (x.ndim - 3) + (QK_ROPE,)
    cos = jnp.cos(angles).reshape(shape).astype(x.dtype)
    sin = jnp.sin(angles).reshape(shape).astype(x.dtype)
    xr1, xr2, xc1, xc2 = jnp.split(x, 4, axis=-1)
    rot = jnp.concatenate([-xr2, xr1, -xc2, xc1], axis=-1)
    return x * cos + rot * sin


def mla_q(cq, g, w_uq, angles):
    B, T, _ = cq.shape
    q = (rmsnorm(cq, g) @ w_uq).reshape(B, T, MLA_HEADS, QK_NOPE + QK_ROPE)
    if angles is None:
        return q
    return jnp.concatenate([q[..., :QK_NOPE], apply_axial_rope(q[..., QK_NOPE:], angles)], axis=-1)


def mla_kv(f, g, w_ukv, angles):
    B, T, _ = f.shape
    ckv, kr = f[..., :KV_RANK], f[..., KV_RANK:]
    kv = (rmsnorm(ckv, g) @ w_ukv).reshape(B, T, MLA_HEADS, QK_NOPE + V_HEAD)
    k_nope, v = kv[..., :QK_NOPE], kv[..., QK_NOPE:]
    if angles is not None:
        kr = apply_axial_rope(kr, angles)
    k = jnp.concatenate([k_nope, jnp.broadcast_to(kr[:, :, None], (B, T, MLA_HEADS, QK_ROPE))], axis=-1)
    return k, v


def attend(q, k, v):
    s = jnp.einsum('bqhd,bkhd->bhqk', q, k).astype(jnp.float32) * MLA_SCALE
    p = jax.nn.softmax(s, axis=-1).astype(v.dtype)
    return jnp.einsum('bhqk,bkhd->bqhd', p, v)


def blocked_attention(q, k, v):
    B, S, H, dk = q.shape
    qb = jnp.moveaxis(q.reshape(B, S // Q_BLOCK, Q_BLOCK, H, dk), 1, 0)
    ob = lax.map(lambda qi: attend(qi, k, v), qb)
    return jnp.moveaxis(ob, 0, 1).reshape(B, S, H * v.shape[-1])


def centred_shift(f, mu):
    pad = jnp.pad(f, ((0, 0), (1, 1), (0, 0)))
    nb = 0.5 * (pad[:, :-2] + pad[:, 2:])
    return f + mu * (nb - f)


def heads(z):
    return z.reshape(z.shape[:-1] + (RWKV_HEADS, RWKV_HEAD))


def rwkv_token_terms(f, h, mu, w0, w2, a0, a2, g2, k_k, k_a, v_first, vres):
    B, T, _ = f.shape
    f = centred_shift(f, mu)
    r, k, v, w_lo, a_lo, g_lo = jnp.split(f, RWKV_SPLITS, axis=-1)
    w_lo = jnp.tanh(w_lo).reshape(B, T, N_DIRS, DECAY_LORA)
    w = -jax.nn.softplus(-(w0 + jnp.einsum('btdl,dlc->btdc', w_lo, w2))) - 0.5
    decay = jnp.exp(-jnp.exp(w.astype(jnp.float32)))
    a = jax.nn.sigmoid(a0 + jnp.einsum('btdl,dlc->btdc', a_lo.reshape(B, T, N_DIRS, AAA_LORA), a2))
    g = jax.nn.sigmoid(g_lo) @ g2
    if vres is not None:
        v0, v1, v2 = vres
        v = v + (v_first - v) * jax.nn.sigmoid(v0 + (h @ v1) @ v2)
    k_dir = k[:, :, None] * (1 + (a - 1) * k_a)
    kkf = heads(k * k_k).astype(jnp.float32)
    kk = (kkf / jnp.maximum(jnp.sqrt(jnp.sum(kkf * kkf, axis=-1, keepdims=True)), 1e-12)).astype(k.dtype)
    return (heads(r), heads(decay), heads(k_dir), heads(v), -kk, kk[:, :, None] * heads(a), g)


def two_way(c_arr, x_arr, per_dir):
    if per_dir:
        cf, cb, xf, xb = c_arr[:, :, 0], c_arr[:, ::-1, 1], x_arr[:, :, 0], x_arr[:, ::-1, 1]
    else:
        cf, cb, xf, xb = c_arr, c_arr[:, ::-1], x_arr, x_arr[:, ::-1]
    s = jnp.stack([jnp.concatenate([cf, xf], axis=1), jnp.concatenate([cb, xb], axis=1)], axis=0)
    return jnp.moveaxis(s, 2, 0).astype(jnp.float32)


def rwkv7_step(state, inp):
    r, w, k, v, a, b = inp
    sa = jnp.einsum('dbhij,dbhj->dbhi', state, a)
    state = state * w[..., None, :] + sa[..., :, None] * b[..., None, :] + v[..., :, None] * k[..., None, :]
    return state, jnp.einsum('dbhij,dbhj->dbhi', state, r)


def bidirectional_rwkv7(tok_c, tok_x):
    r_c, w_c, k_c, v_c, a_c, b_c, _ = tok_c
    r_x, w_x, k_x, v_x, a_x, b_x, _ = tok_x
    xs = (two_way(r_c, r_x, False), two_way(w_c, w_x, True), two_way(k_c, k_x, True),
          two_way(v_c, v_x, False), two_way(a_c, a_x, False), two_way(b_c, b_x, True))
    B, Tc = r_c.shape[:2]
    s0 = jnp.zeros((N_DIRS, B, RWKV_HEADS, RWKV_HEAD, RWKV_HEAD), jnp.float32)
    _, ys = lax.scan(rwkv7_step, s0, xs)
    ys = jnp.moveaxis(ys, 0, 2)
    y_c = ys[0, :, :Tc] + ys[1, :, :Tc][:, ::-1]
    y_x = ys[0, :, Tc:] + ys[1, :, Tc:][:, ::-1]
    return y_c, y_x


def rwkv_output(y, tok, r_k, lnx_w, lnx_b):
    r, _, k_dir, v, _, _, g = tok
    B, T = r.shape[:2]
    mean = jnp.mean(y, axis=-1, keepdims=True)
    var = jnp.mean(jnp.square(y - mean), axis=-1, keepdims=True)
    yn = ((y - mean) * lax.rsqrt(var + LNX_EPS)).reshape(B, T, RWKV_DIM).astype(r.dtype) * lnx_w + lnx_b
    bonus = jnp.sum(r[:, :, None] * k_dir * r_k, axis=(2, 4))[..., None] * v
    return (yn + bonus.reshape(B, T, RWKV_DIM)) * g


def setup_inputs(seed: int = 0) -> dict:
    key = jax.random.key(seed)
    ks = iter(jax.random.split(key, 40))
    nrm = lambda shape, s: jax.random.normal(next(ks), shape, jnp.float32) * s
    uni = lambda shape: jax.random.uniform(next(ks), shape, jnp.float32)
    L, D, R = DEPTH, D_MODEL, RWKV_DIM
    return {
        'x': nrm((BATCH, SEQ, D), 1.0),
        'c': nrm((BATCH, D), 1.0),
        'ctx': nrm((BATCH, CTX_LEN, D), 1.0),
        'c_ctx': nrm((D,), 1.0),
        'ada_w': nrm((L, D, 6 * D), 0.5 * D ** -0.5),
        'ada_b': nrm((L, 6 * D), 0.02),
        'norm1_g': 1.0 + nrm((L, D), 0.02),
        'norm2_g': 1.0 + nrm((L, D), 0.02),
        'w_in': nrm((L, D, C_IN), D ** -0.5),
        'q_norm_g': 1.0 + nrm((L, Q_RANK), 0.02),
        'w_uq': nrm((L, Q_RANK, MLA_HEADS * (QK_NOPE + QK_ROPE)), Q_RANK ** -0.5),
        'kv_norm_g': 1.0 + nrm((L, KV_RANK), 0.02),
        'w_ukv': nrm((L, KV_RANK, MLA_HEADS * (QK_NOPE + V_HEAD)), KV_RANK ** -0.5),
        'shift_mu': uni((L, RWKV_IN)),
        'decay_w0': -6.0 + 5.0 * uni((L, N_DIRS, R)),
        'decay_w2': nrm((L, N_DIRS, DECAY_LORA, R), 0.1 * DECAY_LORA ** -0.5),
        'iclr_a0': nrm((L, N_DIRS, R), 0.1),
        'iclr_a2': nrm((L, N_DIRS, AAA_LORA, R), 0.5 * AAA_LORA ** -0.5),
        'gate_g2': nrm((L, GATE_LORA, R), GATE_LORA ** -0.5),
        'k_k': 0.85 + nrm((L, R), 0.05),
        'k_a': 1.0 + nrm((L, R), 0.05),
        'r_k': nrm((L, RWKV_HEADS, RWKV_HEAD), 0.1),
        'lnx_w': 1.0 + nrm((L, R), 0.02),
        'lnx_b': nrm((L, R), 0.02),
        'vres_v1': nrm((L - 1, D, MV_LORA), 0.5 * D ** -0.5),
        'vres_v2': nrm((L - 1, MV_LORA, R), 0.5 * MV_LORA ** -0.5),
        'vres_v0': nrm((L - 1, R), 0.1),
        'w_out': nrm((L, D_MIX, D), D_MIX ** -0.5),
        'w_mlp1': nrm((L, D, D_FF), D ** -0.5),
        'w_mlp2': nrm((L, D_FF, D), D_FF ** -0.5),
        'final_g': 1.0 + nrm((D,), 0.02),
    }


def reference(x, c, ctx, c_ctx, ada_w, ada_b, norm1_g, norm2_g, w_in, q_norm_g, w_uq, kv_norm_g, w_ukv,
              shift_mu, decay_w0, decay_w2, iclr_a0, iclr_a2, gate_g2, k_k, k_a, r_k, lnx_w, lnx_b,
              vres_v1, vres_v2, vres_v0, w_out, w_mlp1, w_mlp2, final_g):
    B, S, _ = x.shape
    Tc = ctx.shape[1]
    angles = axial_angles(S)
    silu_c = jax.nn.silu(c)
    silu_cc = jax.nn.silu(c_ctx)[None]
    v_first_x = None
    v_first_c = None
    for i in range(DEPTH):
        last = i == DEPTH - 1
        mx = jnp.split(silu_c @ ada_w[i] + ada_b[i], 6, axis=-1)
        mc = jnp.split(silu_cc @ ada_w[i] + ada_b[i], 6, axis=-1)
        hx = modulate(rmsnorm(x, norm1_g[i]), mx[0], mx[1])
        hc = modulate(rmsnorm(ctx, norm1_g[i]), mc[0], mc[1])
        fx = hx @ w_in[i]
        fc = hc @ w_in[i]

        kx, vx = mla_kv(fx[..., Q_RANK:MLA_IN], kv_norm_g[i], w_ukv[i], angles)
        kc, vc = mla_kv(fc[..., Q_RANK:MLA_IN], kv_norm_g[i], w_ukv[i], None)
        qx = mla_q(fx[..., :Q_RANK], q_norm_g[i], w_uq[i], angles)
        att_x = blocked_attention(qx, jnp.concatenate([kc, kx], axis=1), jnp.concatenate([vc, vx], axis=1))

        vres = None if i == 0 else (vres_v0[i - 1], vres_v1[i - 1], vres_v2[i - 1])
        tok_x = rwkv_token_terms(fx[..., MLA_IN:], hx, shift_mu[i], decay_w0[i], decay_w2[i], iclr_a0[i],
                                 iclr_a2[i], gate_g2[i], k_k[i], k_a[i], v_first_x, vres)
        tok_c = rwkv_token_terms(fc[..., MLA_IN:], hc, shift_mu[i], decay_w0[i], decay_w2[i], iclr_a0[i],
                                 iclr_a2[i], gate_g2[i], k_k[i], k_a[i], v_first_c, vres)
        if i == 0:
            v_first_x = tok_x[3].reshape(B, S, RWKV_DIM)
            v_first_c = tok_c[3].reshape(B, Tc, RWKV_DIM)
        y_c, y_x = bidirectional_rwkv7(tok_c, tok_x)
        rw_x = rwkv_output(y_x, tok_x, r_k[i], lnx_w[i], lnx_b[i])

        x = x + mx[2][:, None] * (jnp.concatenate([att_x, rw_x], axis=-1) @ w_out[i])
        x = x + mx[5][:, None] * sq_relu_mlp(modulate(rmsnorm(x, norm2_g[i]), mx[3], mx[4]), w_mlp1[i], w_mlp2[i])

        if not last:
            qc = mla_q(fc[..., :Q_RANK], q_norm_g[i], w_uq[i], None)
            att_c = attend(qc, kc, vc).reshape(B, Tc, MLA_HEADS * V_HEAD)
            rw_c = rwkv_output(y_c, tok_c, r_k[i], lnx_w[i], lnx_b[i])
            ctx = ctx + mc[2][:, None] * (jnp.concatenate([att_c, rw_c], axis=-1) @ w_out[i])
            ctx = ctx + mc[5][:, None] * sq_relu_mlp(modulate(rmsnorm(ctx, norm2_g[i]), mc[3], mc[4]), w_mlp1[i], w_mlp2[i])
    return rmsnorm(x, final_g)
```

```python
import contextlib, math
import numpy as np
import concourse.bass as bass
import concourse.mybir as mybir
from concourse.bass_utils import run_bass_kernel_spmd

F32 = mybir.dt.float32
BF16 = mybir.dt.bfloat16
ALU = mybir.AluOpType
AF = mybir.ActivationFunctionType
AX = mybir.AxisListType
ENGS = ("pe", "dve", "act", "pool", "sp")


def LZ(name, *args, **kwargs):
    return lambda e: getattr(e, name)(*args, **kwargs)


class Buf:
    __slots__ = ("name", "t", "ws", "rd")

    def __init__(self, name, t=None):
        self.name = name; self.t = t; self.ws = []; self.rd = []

    def __getitem__(self, k):
        return self.t[k]


class Op:
    __slots__ = ("eng", "fn", "deps", "is_dma", "idx", "needed", "semval", "sem", "prev_same_sem")

    def __init__(self, eng, fn, is_dma):
        self.eng = eng; self.fn = fn; self.deps = []; self.is_dma = is_dma
        self.needed = False; self.semval = None; self.sem = None; self.prev_same_sem = None


class Sched:
    def __init__(self, nc, n_dma_sems=32):
        self.nc = nc
        self.ops = {e: [] for e in ENGS}
        self.all_ops = []
        self.n_dma_sems = n_dma_sems
        self.es = contextlib.ExitStack()
        self.dma_since_barrier = []

    def sb(self, name, shape, dtype):
        return Buf(name, self.es.enter_context(self.nc.sbuf_tensor(name, list(shape), dtype)))

    def ps(self, name, shape, dtype=F32):
        return Buf(name, self.es.enter_context(self.nc.psum_tensor(name, list(shape), dtype)))

    def op(self, eng, fn, reads=(), writes=(), is_dma=False):
        o = Op(eng, fn, is_dma)
        deps = []
        for b in reads:
            deps.extend(b.ws)
        for b in writes:
            for w in b.ws:
                if w.is_dma and is_dma:
                    continue
                if (not w.is_dma) and (not is_dma) and w.eng == eng:
                    continue
                deps.append(w)
            for r_ in b.rd:
                if (not r_.is_dma) and (not is_dma) and r_.eng == eng:
                    continue
                deps.append(r_)
        seen = set()
        for d in deps:
            if id(d) in seen:
                continue
            seen.add(id(d)); o.deps.append(d)
        for b in reads:
            b.rd.append(o)
        for b in writes:
            if b.rd:
                b.ws = [o]; b.rd = []
            else:
                b.ws.append(o)
                if len(b.ws) > 64:
                    b.ws = b.ws[-64:]
        o.idx = len(self.all_ops)
        self.all_ops.append(o); self.ops[eng].append(o)
        if is_dma:
            self.dma_since_barrier.append(o)
        return o

    def dma(self, eng, out_ap, in_ap, reads=(), writes=(), **kw):
        return self.op(eng, LZ("dma_start", out=out_ap, in_=in_ap, **kw), reads, writes, is_dma=True)

    def barrier(self):
        last = [self.ops[e][-1] for e in ENGS if self.ops[e]]
        dmas = list(reversed(self.dma_since_barrier))
        self.dma_since_barrier = []
        for e in ENGS:
            o = Op(e, None, False)
            o.deps = [d for d in last if d.eng != e and not d.is_dma] + dmas
            o.idx = len(self.all_ops)
            self.all_ops.append(o); self.ops[e].append(o)

    def emit(self, final_waits=()):
        nc = self.nc
        for o in self.all_ops:
            best = {}; nd = []
            for d in o.deps:
                if d.is_dma:
                    nd.append(d)
                elif d.eng not in best or d.idx > best[d.eng].idx:
                    best[d.eng] = d
            o.deps = nd + list(best.values())
            for d in o.deps:
                d.needed = True
        for o in final_waits:
            o.needed = True
        eng_sems = {e: self.es.enter_context(nc.semaphore("s_" + e)) for e in ENGS}
        dma_sems = [self.es.enter_context(nc.semaphore("d%d" % i)) for i in range(self.n_dma_sems)]
        cnt = {e: 0 for e in ENGS}
        dcnt = [0] * self.n_dma_sems
        dlast = [None] * self.n_dma_sems
        k = 0
        for o in self.all_ops:
            if o.fn is None:
                continue
            if o.is_dma:
                s = k % self.n_dma_sems; k += 1
                dcnt[s] += 16
                o.sem = dma_sems[s]; o.semval = dcnt[s]; o.prev_same_sem = dlast[s]; dlast[s] = o
            elif o.needed:
                cnt[o.eng] += 1
                o.sem = eng_sems[o.eng]; o.semval = cnt[o.eng]
        nw = [0]; nwe = {}; nwd = {}
        with nc.Block() as block:
            def make(engname):
                def body(e):
                    known = {}
                    for o in self.ops[engname]:
                        deps = o.deps
                        if o.is_dma and o.prev_same_sem is not None:
                            deps = deps + [o.prev_same_sem]
                        mx = {}
                        for d in deps:
                            if d.sem is None:
                                continue
                            key = id(d.sem)
                            if key not in mx or d.semval > mx[key].semval:
                                mx[key] = d
                        for key, d in mx.items():
                            if known.get(key, 0) >= d.semval:
                                continue
                            e.wait_ge(d.sem, d.semval); nw[0] += 1; nwe[engname] = nwe.get(engname, 0) + 1; nwd[(engname, d.eng, d.is_dma)] = nwd.get((engname, d.eng, d.is_dma), 0) + 1
                            known[key] = d.semval
                        if o.fn is None:
                            continue
                        ins = o.fn(e)
                        if o.sem is not None:
                            ins.then_inc(o.sem, 16 if o.is_dma else 1)
                    if engname == "sp":
                        for o in final_waits:
                            if known.get(id(o.sem), 0) < o.semval:
                                e.wait_ge(o.sem, o.semval); known[id(o.sem)] = o.semval
                return body
            block.tensor(make("pe")); block.vector(make("dve")); block.scalar(make("act"))
            block.gpsimd(make("pool")); block.sync(make("sp"))
        return dict(n_ops={e: len(v) for e, v in self.ops.items()}, n_waits=nw[0], nwe=nwe, nwd=nwd)


D = 1024; SEQ = 2048; TC = 256; T = SEQ + TC; G = T // 256; H = 8; R = 512
CIN = 2336; MLA_IN = 416; DFF = 4096
NORM_EPS = 1e-6; LNX_EPS = 64e-5
MLA_SCALE = 96 ** -0.5
WP = 2308


def gcol(g):
    return 1 if g == 0 else 259 + (g - 1) * 256


def build(NB, L, CH=16, mode=None):
    nc = bass.Bass("TRN2", target_bir_lowering=False)
    fused = mode is None
    has_vres, first, last = (False, True, True) if fused else mode
    NLW = 4 if fused else 1; NLV = 3 if fused else 1
    IQ = 128 // (NB * 8); IL = 64 // IQ; RW = NB + 1
    S = Sched(nc)
    di = {}

    def inp(name, shape, dt=F32):
        di[name] = nc.dram_tensor(name, list(shape), dt, kind="ExternalInput").ap(); return di[name]

    def scr(name, shape, dt=F32):
        return nc.dram_tensor(name, list(shape), dt, kind="Internal").ap()

    c = inp("c", [NB, D]); c_ctx = inp("c_ctx", [1, D])
    if first:
        x = inp("x", [NB, SEQ, D]); ctx = inp("ctx", [NB, TC, D])
    ada_w = inp("ada_w", [NLW, D, 6 * D]); ada_b = inp("ada_b", [NLW, 6 * D])
    norm1_g = inp("norm1_g", [NLW, D]); norm2_g = inp("norm2_g", [NLW, D]); w_in = inp("w_in", [NLW, D, CIN])
    q_norm_g = inp("q_norm_g", [NLW, 256]); w_uq = inp("w_uq", [NLW, 256, 768]); kv_norm_g = inp("kv_norm_g", [NLW, 128])
    w_ukv = inp("w_ukv", [NLW, 128, 1024]); shift_mu = inp("shift_mu", [NLW, 1920])
    decay_w0 = inp("decay_w0", [NLW, 2, R]); decay_w2 = inp("decay_w2", [NLW, 2, 64, R])
    iclr_a0 = inp("iclr_a0", [NLW, 2, R]); iclr_a2 = inp("iclr_a2", [NLW, 2, 64, R]); gate_g2 = inp("gate_g2", [NLW, 128, R])
    k_k = inp("k_k", [NLW, R]); k_a = inp("k_a", [NLW, R]); r_k = inp("r_k", [NLW, R])
    lnx_w = inp("lnx_w", [NLW, R]); lnx_b = inp("lnx_b", [NLW, R])
    vres_v1 = inp("vres_v1", [NLV, D, 32]); vres_v2 = inp("vres_v2", [NLV, 32, R]); vres_v0 = inp("vres_v0", [NLV, R])
    w_out = inp("w_out", [NLW, D, D]); w_mlp1 = inp("w_mlp1", [NLW, D, DFF]); w_mlp2 = inp("w_mlp2", [NLW, DFF, D])
    final_g = inp("final_g", [1, D])
    ident_d = inp("ident", [128, 128]); cos_d = inp("cos_t", [96, T]); sin_d = inp("sin_t", [96, T])
    if last:
        y_out = nc.dram_tensor("y", [NB, SEQ, D], F32, kind="ExternalOutput").ap()

    if last:
        xT = scr("xT", [NB, G, 128, 8 * 256]); vfs = scr("vfs", [NB, G, 128, 4 * 256])
    else:
        xT = nc.dram_tensor("xT_o", [NB, G, 128, 8 * 256], F32, kind="ExternalOutput").ap()
        vfs = nc.dram_tensor("vfs_o", [NB, G, 128, 4 * 256], F32, kind="ExternalOutput").ap()
    if not first:
        xT_i = inp("xT_i", [NB, G, 128, 8 * 256]); vfs_i = inp("vfs_i", [NB, G, 128, 4 * 256])
    hTs = scr("hTs", [NB, 128, 8 * WP], BF16)
    qTs = scr("qTs", [NB, H, 96, T], BF16); kTs = scr("kTs", [NB, H, 96, T], BF16)
    Vs = scr("Vs", [NB, 18, 128, H * 65], BF16)
    rec = scr("rec", [2, T, NB, H, 384]); ysd = scr("ysd", [2, T, NB, H * 64]); bgs = scr("bgs", [NB, T, 1024])
    cats = scr("cats", [NB, G, 128, 8 * 256], BF16)
    us = scr("us", [NB, G, 128, 32 * 256], BF16)

    AW = 44000
    arena_t = S.es.enter_context(nc.sbuf_tensor("arena", [128, AW], F32))
    st = {"top": 0, "n": 0}

    def alloc(name, shape, dt=F32):
        n = int(np.prod(shape[1:])); words = n if dt == F32 else (n + 1) // 2
        words = (words + 7) // 8 * 8
        assert st["top"] + words <= AW, (name, st["top"], words)
        ap = arena_t[0:shape[0], st["top"]:st["top"] + words]
        st["top"] += words; st["n"] += 1
        if dt != F32:
            ap = ap.bitcast(dt)
        ap = ap[:, 0:n]
        if len(shape) == 3:
            ap = ap.rearrange("p (a b) -> p a b", a=shape[1], b=shape[2])
        elif len(shape) == 4:
            ap = ap.rearrange("p (a b c) -> p a b c", a=shape[1], b=shape[2], c=shape[3])
        return Buf("%s_%d" % (name, st["n"]), ap)

    def phase_end():
        S.barrier(); st["top"] = 0

    banks = [S.ps("bank%d" % i, [128, 512], F32) for i in range(8)]
    bst = {"i": 0}

    def bank():
        b = banks[bst["i"] % 8]; bst["i"] += 1; return b

    def V(eng, fn, r, w):
        return S.op(eng, fn, r, w)

    def small_dma(dst, src, w, eng="sp"):
        return S.dma(eng, dst, src, writes=w, allow_slow_non_contiguous=True)

    def fm_vec(dst_buf, dst_ap, tens, off, nchunk):
        small_dma(dst_ap, bass.AP(tens.tensor, off, [[1, 128], [128, nchunk]]), [dst_buf])

    ident = S.sb("ident_sb", [128, 128], F32); identb = S.sb("identb", [128, 128], BF16)
    onesb = S.sb("onesb", [128, 128], BF16); blk = S.sb("blk", [128, 128], BF16)
    epsn = S.sb("epsn", [128, 1], F32); epsl = S.sb("epsl", [128, 1], F32); eps0 = S.sb("eps0", [128, 1], F32)
    S.dma("sp", ident[:], ident_d[:, :], writes=[ident])
    V("act", LZ("copy", out=identb[:], in_=ident[:]), [ident], [identb])
    V("dve", LZ("memset", onesb[:], 1.0), [], [onesb])
    V("dve", LZ("memset", blk[:], 0.0), [], [blk])
    V("dve", LZ("memset", blk[0:64, 0:64], 1.0), [], [blk])
    V("dve", LZ("memset", blk[64:128, 64:128], 1.0), [], [blk])
    V("dve", LZ("memset", epsn[:], NORM_EPS), [], [epsn])
    V("dve", LZ("memset", epsl[:], LNX_EPS), [], [epsl])
    V("dve", LZ("memset", eps0[:], 1e-24), [], [eps0])
    MODS = [S.sb("mods%d" % l, [128, 48, RW], F32) for l in range(L)]
    COEF1 = [S.sb("coef1_%d" % l, [128, 8, RW], F32) for l in range(L)]
    COEF2 = [S.sb("coef2_%d" % l, [128, 8, RW], F32) for l in range(L)]

    scf = alloc("scf", [128, 8, RW]); scb = alloc("scb", [128, 8, RW], BF16)
    for b in range(NB):
        small_dma(scf[:, :, b], bass.AP(c.tensor, b * D, [[1, 128], [128, 8]]), [scf])
    small_dma(scf[:, :, NB], bass.AP(c_ctx.tensor, 0, [[1, 128], [128, 8]]), [scf])
    V("act", LZ("activation", out=scb[:], in_=scf[:], func=AF.Silu), [scf], [scb])
    wst = [alloc("adast", [128, 6144]) for _ in range(2)]
    wbf = [alloc("adabf", [128, 6144], BF16) for _ in range(2)]
    adab = alloc("adab", [128, 48]); g1t = alloc("g1t", [128, 8]); g2t = alloc("g2t", [128, 8]); tmp8 = alloc("tmp8", [128, 8, RW])
    for l in range(L):
        acc = MODS[l]
        for k in range(8):
            ws_, wb_ = wst[k % 2], wbf[k % 2]
            S.dma("sp", ws_[:], ada_w[l, k * 128:(k + 1) * 128, :], writes=[ws_])
            V("act" if k % 2 else "dve", (LZ("copy", out=wb_[:], in_=ws_[:])) if k % 2 else
              (LZ("tensor_copy", out=wb_[:], in_=ws_[:])), [ws_], [wb_])
            pb = bank()
            pv = pb[:, 0:48 * 8].rearrange("p (a b) -> p a b", a=48, b=8)
            for m in range(48):
                V("pe", LZ("matmul", pv[:, m, 0:RW], lhsT=wb_[:, m * 128:(m + 1) * 128],
                                                                      rhs=scb[:, k, :], start=True, stop=True), [wb_, scb], [pb])
            if k == 0:
                V("dve", LZ("tensor_copy", out=acc[:], in_=pv[:, :, 0:RW]), [pb], [acc])
            else:
                V("dve", LZ("tensor_tensor", out=acc[:], in0=acc[:], in1=pv[:, :, 0:RW], op=ALU.add), [pb, acc], [acc])
        fm_vec(adab, adab[:], ada_b, l * 6144, 48)
        fm_vec(g1t, g1t[:], norm1_g, l * D, 8); fm_vec(g2t, g2t[:], norm2_g, l * D, 8)
        V("dve", LZ("tensor_tensor", out=acc[:], in0=acc[:], in1=adab[:].unsqueeze(2).to_broadcast([128, 48, RW]), op=ALU.add), [acc, adab], [acc])
        for (co, gt, lo) in ((COEF1[l], g1t, 8), (COEF2[l], g2t, 32)):
            V("dve", LZ("tensor_scalar", out=tmp8[:], in0=acc[:, lo:lo + 8, :], scalar1=1.0, scalar2=None, op0=ALU.add), [acc], [tmp8])
            V("dve", LZ("tensor_tensor", out=co[:], in0=tmp8[:], in1=gt[:].unsqueeze(2).to_broadcast([128, 8, RW]), op=ALU.mult), [tmp8, gt], [co])
    phase_end()

    zt = alloc("zt", [128, 8, 2], BF16)
    V("dve", LZ("memset", zt[:], 0.0), [], [zt])
    for b in range(NB):
        for (col, n) in ((0, 1), (257, 2), (2307, 1)):
            small_dma(bass.AP(hTs.tensor, b * 128 * 8 * WP + col, [[8 * WP, 128], [WP, 8], [1, n]]), zt[:, :, 0:n], [], eng="pool")
    if first:
        tin = [alloc("tin", [128, D]) for _ in range(2)]
        xtt = [alloc("xtt", [128, 8, 256]) for _ in range(2)]
        it = 0
        for b in range(NB):
            for g in range(G):
                xo = xtt[(b * G + g) % 2]
                for tt in range(2):
                    ti = tin[it % 2]; it += 1
                    src = ctx[b, tt * 128:(tt + 1) * 128, :] if g == 0 else x[b, (g - 1) * 256 + tt * 128:(g - 1) * 256 + (tt + 1) * 128, :]
                    S.dma("sp", ti[:], src, writes=[ti])
                    for hf in range(2):
                        pb = bank()
                        for j in range(4):
                            kk = hf * 4 + j
                            V("pe", LZ("transpose", out=pb[:, j * 128:(j + 1) * 128], in_=ti[:, kk * 128:(kk + 1) * 128], identity=ident[:]), [ti, ident], [pb])
                        V("act" if hf else "dve", (LZ("copy", out=xo[:, hf * 4:hf * 4 + 4, tt * 128:(tt + 1) * 128], in_=pb[:].rearrange("p (a b) -> p a b", a=4, b=128))) if hf else
                          (LZ("tensor_copy", out=xo[:, hf * 4:hf * 4 + 4, tt * 128:(tt + 1) * 128], in_=pb[:].rearrange("p (a b) -> p a b", a=4, b=128))), [pb], [xo])
                S.dma("pool", xT[b, g].rearrange("p (a b) -> p a b", a=8, b=256), xo[:], reads=[xo])
        phase_end()
    else:
        for b in range(NB):
            S.dma("sp", xT[b], xT_i[b]); S.dma("sp", vfs[b], vfs_i[b])
        phase_end()

    def rms_fm(xin, nk, width, inv_n, sq, rstd, tmp):
        xb, xap = xin
        V("act", LZ("activation", out=sq[:, 0:nk, 0:width], in_=xap, func=AF.Square), [xb], [sq])
        pb = bank()
        for k in range(nk):
            V("pe", LZ("matmul", pb[:, 0:width], lhsT=onesb[:], rhs=sq[:, k, 0:width], start=(k == 0), stop=(k == nk - 1)), [sq, onesb], [pb])
        V("act", LZ("activation", out=tmp[:, 0:width], in_=pb[:, 0:width], func=AF.Sqrt, bias=epsn[:], scale=inv_n), [pb, epsn], [tmp])
        V("dve", LZ("reciprocal", out=rstd[:, 0:width], in_=tmp[:, 0:width]), [tmp], [rstd])

    out_dmas = []
    for l in range(L):
        xt2 = [alloc("xa", [128, 8, 256]) for _ in range(2)]
        sq = alloc("sq", [128, 8, 256], BF16); rstd = alloc("rstd", [128, 256]); tmpa = alloc("tmpa", [128, 256])
        tn = alloc("tn", [128, 8, 256]); hb = [alloc("hb", [128, 8, 256], BF16) for _ in range(2)]
        for b in range(NB):
            for g in range(G):
                xa = xt2[g % 2]; ho = hb[g % 2]; row = NB if g == 0 else b
                S.dma("sp", xa[:], xT[b, g].rearrange("p (a b) -> p a b", a=8, b=256), writes=[xa])
                rms_fm((xa, xa[:]), 8, 256, 1.0 / D, sq, rstd, tmpa)
                V("dve", LZ("tensor_tensor", out=tn[:], in0=xa[:], in1=rstd[:].unsqueeze(1).to_broadcast([128, 8, 256]), op=ALU.mult), [xa, rstd], [tn])
                for k in range(8):
                    V("act" if k % 2 else "pool",
                      (LZ("activation", out=ho[:, k, :], in_=tn[:, k, :], func=AF.Identity, bias=MODS[l][:, k, row:row + 1], scale=COEF1[l][:, k, row:row + 1])) if k % 2 else
                      (LZ("tensor_scalar", out=ho[:, k, :], in0=tn[:, k, :], scalar1=COEF1[l][:, k, row:row + 1], scalar2=MODS[l][:, k, row:row + 1], op0=ALU.mult, op1=ALU.add)),
                      [tn, MODS[l], COEF1[l]], [ho])
                S.dma("pool", bass.AP(hTs.tensor, b * 128 * 8 * WP + gcol(g), [[8 * WP, 128], [WP, 8], [1, 256]]), ho[:], reads=[ho])
        phase_end()

        stg = alloc("stg", [128, CIN])
        Wm = alloc("Wm", [128, 8, 384], BF16); Wkr = alloc("Wkr", [128, 8, 96], BF16); Wkrot = alloc("Wkrot", [128, 8, 96], BF16)
        Wuq = alloc("Wuq", [128, 2, 768], BF16); Wuqr = alloc("Wuqr", [128, 2, 8, 96], BF16); Wukv = alloc("Wukv", [128, 1024], BF16)
        gq = alloc("gq", [128, 2]); gkv = alloc("gkv", [128, 1])
        V("dve", LZ("memset", Wkr[:], 0.0), [], [Wkr]); V("dve", LZ("memset", Wkrot[:], 0.0), [], [Wkrot])
        V("dve", LZ("memset", Wuqr[:], 0.0), [], [Wuqr])
        for k in range(8):
            S.dma("sp", stg[:, 0:MLA_IN], w_in[l, k * 128:(k + 1) * 128, 0:MLA_IN], writes=[stg])
            V("act", LZ("copy", out=Wm[:, k, :], in_=stg[:, 0:384]), [stg], [Wm])
            V("dve", LZ("tensor_copy", out=Wkr[:, k, 64:96], in_=stg[:, 384:416]), [stg], [Wkr])
            for (do, so, sg) in ((64, 392, -1.0), (72, 384, 1.0), (80, 408, -1.0), (88, 400, 1.0)):
                V("dve", LZ("tensor_scalar", out=Wkrot[:, k, do:do + 8], in0=stg[:, so:so + 8], scalar1=sg, scalar2=None, op0=ALU.mult), [stg], [Wkrot])
        for k in range(2):
            S.dma("sp", stg[:, 0:768], w_uq[l, k * 128:(k + 1) * 128, :], writes=[stg])
            V("act", LZ("copy", out=Wuq[:, k, :], in_=stg[:, 0:768]), [stg], [Wuq])
            sv = stg[:, 0:768].rearrange("p (h d) -> p h d", h=8, d=96)
            for (do, so, sg) in ((64, 72, -1.0), (72, 64, 1.0), (80, 88, -1.0), (88, 80, 1.0)):
                V("dve", LZ("tensor_scalar", out=Wuqr[:, k, :, do:do + 8], in0=sv[:, :, so:so + 8], scalar1=sg, scalar2=None, op0=ALU.mult), [stg], [Wuqr])
        S.dma("sp", stg[:, 0:1024], w_ukv[l, :, :], writes=[stg])
        V("act", LZ("copy", out=Wukv[:], in_=stg[:, 0:1024]), [stg], [Wukv])
        fm_vec(gq, gq[:], q_norm_g, l * 256, 2); fm_vec(gkv, gkv[:], kv_norm_g, l * 128, 1)
        hin = [alloc("hin", [128, 8, 256], BF16) for _ in range(2)]
        cost = alloc("cost", [96, 256]); sint = alloc("sint", [96, 256])
        cqs = alloc("cqs", [128, 2, 256]); sq2 = alloc("sq2", [128, 2, 256], BF16); rs2 = alloc("rs2", [128, 256]); tm2 = alloc("tm2", [128, 256])
        cqn = alloc("cqn", [128, 2, 256], BF16); ckvn = alloc("ckvn", [128, 256], BF16); ckvs = alloc("ckvs", [128, 256])
        qo = alloc("qo", [96, 8, 256], BF16); ko = alloc("ko", [96, 8, 256], BF16); vo = alloc("vo", [128, 2, H * 65], BF16)
        t1 = alloc("t1", [96, 256]); t2 = alloc("t2", [96, 256]); krp = alloc("krp", [96, 256], BF16)
        V("dve", LZ("memset", vo[:], 1.0), [], [vo])
        for b in range(NB):
            for g in range(G):
                hi = hin[g % 2]; c0 = (0 if g == 0 else 256 + (g - 1) * 256)
                S.dma("sp", hi[:], bass.AP(hTs.tensor, b * 128 * 8 * WP + gcol(g), [[8 * WP, 128], [WP, 8], [1, 256]]), writes=[hi])
                S.dma("sp", cost[64:96, :], cos_d[64:96, c0:c0 + 256], writes=[cost]); S.dma("sp", sint[64:96, :], sin_d[64:96, c0:c0 + 256], writes=[sint])
                for m in range(3):
                    pb = bank()
                    for k in range(8):
                        V("pe", LZ("matmul", pb[:, 0:256], lhsT=Wm[:, k, m * 128:(m + 1) * 128], rhs=hi[:, k, :], start=(k == 0), stop=(k == 7)), [Wm, hi], [pb])
                    if m < 2:
                        V("dve", LZ("tensor_copy", out=cqs[:, m, :], in_=pb[:, 0:256]), [pb], [cqs])
                    else:
                        V("dve", LZ("tensor_copy", out=ckvs[:], in_=pb[:, 0:256]), [pb], [ckvs])
                rms_fm((cqs, cqs[:]), 2, 256, 1.0 / 256, sq2, rs2, tm2)
                for m in range(2):
                    V("dve", LZ("scalar_tensor_tensor", out=cqn[:, m, :], in0=cqs[:, m, :], scalar=gq[:, m:m + 1], in1=rs2[:], op0=ALU.mult, op1=ALU.mult), [cqs, gq, rs2], [cqn])
                rms_fm((ckvs, ckvs[:].unsqueeze(1)), 1, 256, 1.0 / 128, sq2, rs2, tm2)
                V("dve", LZ("scalar_tensor_tensor", out=ckvn[:], in0=ckvs[:], scalar=gkv[:, 0:1], in1=rs2[:], op0=ALU.mult, op1=ALU.mult), [ckvs, gkv, rs2], [ckvn])
                pk = bank(); pr = bank()
                for k in range(8):
                    V("pe", LZ("matmul", pk[0:96, 0:256], lhsT=Wkr[:, k, :], rhs=hi[:, k, :], start=(k == 0), stop=(k == 7)), [Wkr, hi], [pk])
                for k in range(8):
                    V("pe", LZ("matmul", pr[0:96, 0:256], lhsT=Wkrot[:, k, :], rhs=hi[:, k, :], start=(k == 0), stop=(k == 7)), [Wkrot, hi], [pr])
                V("dve", LZ("tensor_tensor", out=t1[64:96, :], in0=pk[64:96, 0:256], in1=cost[64:96, :], op=ALU.mult), [pk, cost], [t1])
                V("dve", LZ("tensor_tensor", out=t2[64:96, :], in0=pr[64:96, 0:256], in1=sint[64:96, :], op=ALU.mult), [pr, sint], [t2])
                V("dve", LZ("tensor_tensor", out=krp[64:96, :], in0=t1[64:96, :], in1=t2[64:96, :], op=ALU.add), [t1, t2], [krp])
                V("pool", LZ("tensor_copy", out=ko[64:96, :, :], in_=krp[64:96, :].unsqueeze(1).to_broadcast([32, 8, 256])), [krp], [ko])
                for h in range(H):
                    pb = bank()
                    V("pe", LZ("matmul", pb[0:64, 0:256], lhsT=Wukv[:, h * 128:h * 128 + 64], rhs=ckvn[:], start=True, stop=True), [Wukv, ckvn], [pb])
                    V("act", LZ("copy", out=ko[0:64, h, :], in_=pb[0:64, 0:256]), [pb], [ko])
                    pq = bank(); pr2 = bank()
                    for k in range(2):
                        V("pe", LZ("matmul", pq[0:96, 0:256], lhsT=Wuq[:, k, h * 96:(h + 1) * 96], rhs=cqn[:, k, :], start=(k == 0), stop=(k == 1)), [Wuq, cqn], [pq])
                    for k in range(2):
                        V("pe", LZ("matmul", pr2[0:96, 0:256], lhsT=Wuqr[:, k, h, :], rhs=cqn[:, k, :], start=(k == 0), stop=(k == 1)), [Wuqr, cqn], [pr2])
                    V("act", LZ("copy", out=qo[0:64, h, :], in_=pq[0:64, 0:256]), [pq], [qo])
                    V("dve", LZ("tensor_tensor", out=t1[64:96, :], in0=pq[64:96, 0:256], in1=cost[64:96, :], op=ALU.mult), [pq, cost], [t1])
                    V("dve", LZ("tensor_tensor", out=t2[64:96, :], in0=pr2[64:96, 0:256], in1=sint[64:96, :], op=ALU.mult), [pr2, sint], [t2])
                    V("dve", LZ("tensor_tensor", out=qo[64:96, h, :], in0=t1[64:96, :], in1=t2[64:96, :], op=ALU.add), [t1, t2], [qo])
                for tt in range(2):
                    for hf in range(2):
                        pb = bank()
                        V("pe", LZ("matmul", pb[:, 0:512], lhsT=ckvn[:, tt * 128:(tt + 1) * 128], rhs=Wukv[:, hf * 512:(hf + 1) * 512], start=True, stop=True), [ckvn, Wukv], [pb])
                        V("act", LZ("copy", out=vo[:, tt, :].rearrange("p (h d) -> p h d", h=8, d=65)[:, hf * 4:hf * 4 + 4, 0:64],
                                                                         in_=pb[:, 0:512].rearrange("p (h d) -> p h d", h=4, d=128)[:, :, 64:128]), [pb], [vo])
                S.dma("pool", bass.AP(qTs.tensor, b * H * 96 * T + c0, [[T, 96], [96 * T, 8], [1, 256]]), qo[:], reads=[qo])
                S.dma("pool", bass.AP(kTs.tensor, b * H * 96 * T + c0, [[T, 96], [96 * T, 8], [1, 256]]), ko[:], reads=[ko])
                S.dma("pool", bass.AP(Vs.tensor, (b * 18 + 2 * g) * 128 * H * 65, [[H * 65, 128], [128 * H * 65, 2], [1, H * 65]]), vo[:], reads=[vo])
        phase_end()

        Wr = alloc("Wr", [128, 8, 1920], BF16); Wmu = alloc("Wmu", [128, 8, 1920], BF16)
        stg = alloc("stg", [128, 1920]); mub = alloc("mub", [128, 1920])
        S.dma("sp", mub[:], bass.AP(shift_mu.tensor, l * 1920, [[0, 128], [1, 1920]]), writes=[mub])
        for k in range(8):
            S.dma("sp", stg[:], w_in[l, k * 128:(k + 1) * 128, MLA_IN:CIN], writes=[stg])
            V("act", LZ("copy", out=Wr[:, k, :], in_=stg[:]), [stg], [Wr])
            V("dve", LZ("tensor_tensor", out=Wmu[:, k, :], in0=stg[:], in1=mub[:], op=ALU.mult), [stg, mub], [Wmu])
        Wd2 = alloc("Wd2", [128, 512], BF16); Wa2 = alloc("Wa2", [128, 512], BF16); Wg2 = alloc("Wg2", [128, 512], BF16)
        Wv1 = alloc("Wv1", [128, 8, 32], BF16); Wv2 = alloc("Wv2", [32, 512], BF16)
        for (wt, src) in ((Wd2, decay_w2[l].rearrange("d l c -> (d l) c")), (Wa2, iclr_a2[l].rearrange("d l c -> (d l) c")), (Wg2, gate_g2[l])):
            S.dma("sp", stg[:, 0:512], src, writes=[stg])
            V("act", LZ("copy", out=wt[:], in_=stg[:, 0:512]), [stg], [wt])
        vres_on = (l > 0) if fused else has_vres
        vl = (l - 1) if fused else 0
        if vres_on:
            S.dma("sp", stg[:, 0:256].rearrange("p (a b) -> p a b", a=8, b=32), bass.AP(vres_v1.tensor, vl * D * 32, [[32, 128], [128 * 32, 8], [1, 32]]), writes=[stg])
            V("act", LZ("copy", out=Wv1[:], in_=stg[:, 0:256].rearrange("p (a b) -> p a b", a=8, b=32)), [stg], [Wv1])
            S.dma("sp", stg[0:32, 0:512], vres_v2[vl], writes=[stg])
            V("act", LZ("copy", out=Wv2[:], in_=stg[0:32, 0:512]), [stg], [Wv2])
        pv = alloc("pvec", [128, 12, 4])
        for d in range(2):
            fm_vec(pv, pv[:, d, :], decay_w0, (l * 2 + d) * R, 4); fm_vec(pv, pv[:, 2 + d, :], iclr_a0, (l * 2 + d) * R, 4)
        fm_vec(pv, pv[:, 4, :], k_k, l * R, 4); fm_vec(pv, pv[:, 5, :], k_a, l * R, 4); fm_vec(pv, pv[:, 8, :], r_k, l * R, 4)
        if vres_on:
            fm_vec(pv, pv[:, 7, :], vres_v0, vl * R, 4)
        V("dve", LZ("tensor_scalar", out=pv[:, 6, :], in0=pv[:, 5, :], scalar1=-1.0, scalar2=1.0, op0=ALU.mult, op1=ALU.add), [pv], [pv])
        hin = [alloc("hin", [128, 8, 258], BF16) for _ in range(2)]
        dh = alloc("dh", [128, 8, 256], BF16); dht = alloc("dht", [128, 8, 256])
        nm = ["r", "k", "v", "kk", "rn", "sg", "w0", "w1", "a0", "a1", "kd0", "kd1", "b0", "b1", "g", "bon", "t", "vf"]
        Fm = {n: alloc("f_" + n, [128, 256]) for n in nm}
        wlo = alloc("wlo", [128, 256], BF16); alo = alloc("alo", [128, 256], BF16); glo = alloc("glo", [128, 256], BF16)
        vlo = alloc("vlo", [32, 256], BF16); sqk = alloc("sqk", [128, 256], BF16); tb = alloc("tb", [128, 256], BF16)
        recs = [alloc("rect", [128, 2, 384]) for _ in range(4)]; bgts = [alloc("bgt", [128, 2, 128]) for _ in range(2)]
        NEG = -math.exp(-0.5)
        for b in range(NB):
            for g in range(G):
                hi = hin[g % 2]; pos0 = (0 if g == 0 else 256 + (g - 1) * 256)
                S.dma("sp", hi[:], bass.AP(hTs.tensor, b * 128 * 8 * WP + gcol(g) - 1, [[8 * WP, 128], [WP, 8], [1, 258]]), writes=[hi])
                V("pool", LZ("tensor_tensor", out=dht[:], in0=hi[:, :, 0:256], in1=hi[:, :, 2:258], op=ALU.add), [hi], [dht])
                V("dve", LZ("scalar_tensor_tensor", out=dh[:], in0=dht[:], scalar=0.5, in1=hi[:, :, 1:257], op0=ALU.mult, op1=ALU.subtract), [dht, hi], [dh])

                def proj(m, hi=hi):
                    pb = bank()
                    for k in range(8):
                        V("pe", LZ("matmul", pb[:, 0:256], lhsT=Wr[:, k, m * 128:(m + 1) * 128], rhs=hi[:, k, 1:257], start=(k == 0), stop=False), [Wr, hi], [pb])
                    for k in range(8):
                        V("pe", LZ("matmul", pb[:, 0:256], lhsT=Wmu[:, k, m * 128:(m + 1) * 128], rhs=dh[:, k, :], start=False, stop=(k == 7)), [Wmu, dh], [pb])
                    return pb
                pb = proj(12); V("act", LZ("activation", out=wlo[:], in_=pb[:, 0:256], func=AF.Tanh), [pb], [wlo])
                pb = proj(13); V("act", LZ("copy", out=alo[:], in_=pb[:, 0:256]), [pb], [alo])
                pb = proj(14); V("act", LZ("activation", out=glo[:], in_=pb[:, 0:256], func=AF.Sigmoid), [pb], [glo])
                if vres_on:
                    pb = bank()
                    for k in range(8):
                        V("pe", LZ("matmul", pb[0:32, 0:256], lhsT=Wv1[:, k, :], rhs=hi[:, k, 1:257], start=(k == 0), stop=(k == 7)), [Wv1, hi], [pb])
                    V("act", LZ("copy", out=vlo[:], in_=pb[0:32, 0:256]), [pb], [vlo])
                for cc in range(4):
                    f = Fm
                    for (n, m) in (("r", cc), ("k", 4 + cc), ("v", 8 + cc)):
                        pb = proj(m)
                        V("act" if n != "k" else "dve", (LZ("copy", out=f[n][:], in_=pb[:, 0:256])) if n != "k" else
                          (LZ("tensor_copy", out=f[n][:], in_=pb[:, 0:256])), [pb], [f[n]])
                    for d in range(2):
                        pb = bank()
                        V("pe", LZ("matmul", pb[:, 0:256], lhsT=Wd2[64 * d:64 * d + 64, cc * 128:(cc + 1) * 128], rhs=wlo[64 * d:64 * d + 64, :], start=True, stop=True), [Wd2, wlo], [pb])
                        V("act", LZ("activation", out=f["sg"][:], in_=pb[:, 0:256], func=AF.Sigmoid, bias=pv[:, d, cc:cc + 1], scale=1.0), [pb, pv], [f["sg"]])
                        V("act", LZ("activation", out=f["w%d" % d][:], in_=f["sg"][:], func=AF.Exp, scale=NEG), [f["sg"]], [f["w%d" % d]])
                        pb = bank()
                        V("pe", LZ("matmul", pb[:, 0:256], lhsT=Wa2[64 * d:64 * d + 64, cc * 128:(cc + 1) * 128], rhs=alo[64 * d:64 * d + 64, :], start=True, stop=True), [Wa2, alo], [pb])
                        V("act", LZ("activation", out=f["a%d" % d][:], in_=pb[:, 0:256], func=AF.Sigmoid, bias=pv[:, 2 + d, cc:cc + 1], scale=1.0), [pb, pv], [f["a%d" % d]])
                    pb = bank()
                    V("pe", LZ("matmul", pb[:, 0:256], lhsT=Wg2[:, cc * 128:(cc + 1) * 128], rhs=glo[:], start=True, stop=True), [Wg2, glo], [pb])
                    V("act", LZ("copy", out=f["g"][:], in_=pb[:, 0:256]), [pb], [f["g"]])
                    vfd = bass.AP(vfs.tensor, ((b * G + g) * 128) * 1024 + cc * 256, [[1024, 128], [1, 256]])
                    if not vres_on:
                        S.dma("pool", vfd, f["v"][:], reads=[f["v"]])
                    else:
                        S.dma("sp", f["vf"][:], vfd, writes=[f["vf"]])
                        pb = bank()
                        V("pe", LZ("matmul", pb[:, 0:256], lhsT=Wv2[:, cc * 128:(cc + 1) * 128], rhs=vlo[:], start=True, stop=True), [Wv2, vlo], [pb])
                        V("act", LZ("activation", out=f["sg"][:], in_=pb[:, 0:256], func=AF.Sigmoid, bias=pv[:, 7, cc:cc + 1], scale=1.0), [pb, pv], [f["sg"]])
                        V("dve", LZ("tensor_tensor", out=f["t"][:], in0=f["vf"][:], in1=f["v"][:], op=ALU.subtract), [f["vf"], f["v"]], [f["t"]])
                        V("dve", LZ("tensor_tensor", out=f["t"][:], in0=f["t"][:], in1=f["sg"][:], op=ALU.mult), [f["t"], f["sg"]], [f["t"]])
                        V("dve", LZ("tensor_tensor", out=f["v"][:], in0=f["v"][:], in1=f["t"][:], op=ALU.add), [f["v"], f["t"]], [f["v"]])
                    V("dve", LZ("tensor_scalar", out=f["kk"][:], in0=f["k"][:], scalar1=pv[:, 4, cc:cc + 1], scalar2=None, op0=ALU.mult), [f["k"], pv], [f["kk"]])
                    V("act", LZ("activation", out=sqk[:], in_=f["kk"][:], func=AF.Square), [f["kk"]], [sqk])
                    pb = bank()
                    V("pe", LZ("matmul", pb[:, 0:256], lhsT=blk[:], rhs=sqk[:], start=True, stop=True), [blk, sqk], [pb])
                    V("act", LZ("activation", out=f["rn"][:], in_=pb[:, 0:256], func=AF.Sqrt, bias=eps0[:], scale=1.0), [pb, eps0], [f["rn"]])
                    V("dve", LZ("reciprocal", out=f["rn"][:], in_=f["rn"][:]), [f["rn"]], [f["rn"]])
                    V("dve", LZ("tensor_tensor", out=f["kk"][:], in0=f["kk"][:], in1=f["rn"][:], op=ALU.mult), [f["kk"], f["rn"]], [f["kk"]])
                    for d in range(2):
                        V("pool", LZ("tensor_scalar", out=f["t"][:], in0=f["a%d" % d][:], scalar1=pv[:, 5, cc:cc + 1], scalar2=pv[:, 6, cc:cc + 1], op0=ALU.mult, op1=ALU.add), [f["a%d" % d], pv], [f["t"]])
                        V("pool", LZ("tensor_tensor", out=f["kd%d" % d][:], in0=f["k"][:], in1=f["t"][:], op=ALU.mult), [f["k"], f["t"]], [f["kd%d" % d]])
                        V("pool", LZ("tensor_tensor", out=f["b%d" % d][:], in0=f["kk"][:], in1=f["a%d" % d][:], op=ALU.mult), [f["kk"], f["a%d" % d]], [f["b%d" % d]])
                    V("dve", LZ("tensor_scalar", out=f["kk"][:], in0=f["kk"][:], scalar1=-1.0, scalar2=None, op0=ALU.mult), [f["kk"]], [f["kk"]])
                    V("dve", LZ("tensor_tensor", out=f["t"][:], in0=f["kd0"][:], in1=f["kd1"][:], op=ALU.add), [f["kd0"], f["kd1"]], [f["t"]])
                    V("dve", LZ("scalar_tensor_tensor", out=tb[:], in0=f["r"][:], scalar=pv[:, 8, cc:cc + 1], in1=f["t"][:], op0=ALU.mult, op1=ALU.mult), [f["r"], pv, f["t"]], [tb])
                    pb = bank()
                    V("pe", LZ("matmul", pb[:, 0:256], lhsT=blk[:], rhs=tb[:], start=True, stop=True), [blk, tb], [pb])
                    V("dve", LZ("tensor_tensor", out=f["bon"][:], in0=pb[:, 0:256], in1=f["v"][:], op=ALU.mult), [pb, f["v"]], [f["bon"]])
                    for tt in range(2):
                        for d in range(2):
                            fl = ["kk", "r", "w%d" % d, "kd%d" % d, "b%d" % d, "v"]
                            pbs = [bank(), bank()]
                            for fi, n in enumerate(fl):
                                pb = pbs[fi // 3]
                                V("pe", LZ("transpose", out=pb[:, (fi % 3) * 128:(fi % 3 + 1) * 128], in_=f[n][:, tt * 128:(tt + 1) * 128], identity=ident[:]), [f[n], ident], [pb])
                            rt = recs[tt * 2 + d]
                            for hf in range(2):
                                pb = pbs[hf]
                                for hh in range(2):
                                    o_ap = rt[:, hh, hf * 192:hf * 192 + 192].rearrange("p (f j) -> p f j", f=3, j=64)
                                    i_ap = pb[:, 0:384].rearrange("p (f h j) -> p f h j", f=3, h=2, j=64)[:, :, hh, :]
                                    if hh:
                                        V("act", LZ("copy", out=o_ap, in_=i_ap), [pb], [rt])
                                    else:
                                        V("dve", LZ("tensor_copy", out=o_ap, in_=i_ap), [pb], [rt])
                            S.dma("pool", bass.AP(rec.tensor, ((d * T + pos0 + tt * 128) * NB + b) * H * 384 + 2 * cc * 384, [[NB * H * 384, 128], [1, 768]]),
                                  rt[:].rearrange("p h f -> p (h f)"), reads=[rt])
                        pb = bank(); bt = bgts[tt]
                        V("pe", LZ("transpose", out=pb[:, 0:128], in_=f["bon"][:, tt * 128:(tt + 1) * 128], identity=ident[:]), [f["bon"], ident], [pb])
                        V("pe", LZ("transpose", out=pb[:, 128:256], in_=f["g"][:, tt * 128:(tt + 1) * 128], identity=ident[:]), [f["g"], ident], [pb])
                        V("act", LZ("copy", out=bt[:], in_=pb[:, 0:256].rearrange("p (a b) -> p a b", a=2, b=128)), [pb], [bt])
                        S.dma("pool", bass.AP(bgs.tensor, (b * T + pos0 + tt * 128) * 1024 + cc * 128, [[1024, 128], [512, 2], [1, 128]]), bt[:], reads=[bt])
        phase_end()

        banksA = banks[0:4]; accb = banks[4:8]; ai = [0]

        def bankA():
            bb = banksA[ai[0] % 4]; ai[0] += 1; return bb
        Vt = alloc("Vt", [128, 18, H * 65], BF16); att = alloc("att", [128, 18, 512], BF16)
        qh = [alloc("qh", [96, T], BF16) for _ in range(2)]; kh = [alloc("kh", [96, T], BF16) for _ in range(2)]
        pts = [alloc("pt", [128, 512], BF16) for _ in range(3)]; rcp = alloc("rcp", [128, 4]); catt = [alloc("catt", [128, 4, 256], BF16) for _ in range(2)]
        pti = 0
        for b in range(NB):
            S.dma("sp", Vt[:], bass.AP(Vs.tensor, b * 18 * 128 * H * 65, [[H * 65, 128], [128 * H * 65, 18], [1, H * 65]]), writes=[Vt])
            for h in range(H):
                q_ = qh[h % 2]; k_ = kh[h % 2]
                S.dma("sp", q_[:], qTs[b, h], writes=[q_]); S.dma("sp", k_[:], kTs[b, h], writes=[k_])
                for (q0, qw, kts) in [(0, 256, 2)] + [(256 + i * 512, 512, 18) for i in range(4)]:
                    nq = qw // 128
                    for kt in range(kts):
                        pb = bankA(); pt = pts[pti % 3]; pti += 1
                        V("pe", LZ("matmul", pb[:, 0:qw], lhsT=k_[:, kt * 128:(kt + 1) * 128], rhs=q_[:, q0:q0 + qw], start=True, stop=True), [k_, q_], [pb])
                        V("act", LZ("activation", out=pt[:, 0:qw], in_=pb[:, 0:qw], func=AF.Exp, scale=MLA_SCALE), [pb], [pt])
                        for qi in range(nq):
                            V("pe", LZ("matmul", accb[qi][:, 0:65], lhsT=pt[:, qi * 128:(qi + 1) * 128], rhs=Vt[:, kt, h * 65:(h + 1) * 65], start=(kt == 0), stop=(kt == kts - 1)), [pt, Vt], [accb[qi]])
                    for qi in range(nq):
                        qt = q0 // 128 + qi
                        V("dve", LZ("reciprocal", out=rcp[:, qi:qi + 1], in_=accb[qi][:, 64:65]), [accb[qi]], [rcp])
                        V("act", LZ("activation", out=att[:, qt, h * 64:(h + 1) * 64], in_=accb[qi][:, 0:64], func=AF.Copy, scale=rcp[:, qi:qi + 1]), [accb[qi], rcp], [att])
            for g in range(G):
                ct = catt[g % 2]
                for tt in range(2):
                    pb = bankA(); pbv = pb[:].bitcast(BF16)
                    for cc in range(4):
                        V("pe", LZ("transpose", out=pbv[:, cc * 128:(cc + 1) * 128], in_=att[:, 2 * g + tt, cc * 128:(cc + 1) * 128], identity=identb[:]), [att, identb], [pb])
                    V("dve", LZ("tensor_copy", out=ct[:, :, tt * 128:(tt + 1) * 128], in_=pbv[:, 0:512].rearrange("p (a b) -> p a b", a=4, b=128)), [pb], [ct])
                S.dma("pool", bass.AP(cats.tensor, (b * G + g) * 128 * 2048, [[2048, 128], [256, 4], [1, 256]]), ct[:], reads=[ct])
        phase_end()

        NBH = NB * 8
        Sst = alloc("Sst", [128, 2, IL, 64]); tmp = alloc("stmp", [128, 2, IL, 64]); tmp2 = alloc("stmp2", [128, 2, IL, 64]); sa = alloc("sa", [128, 2, IL])
        cks = [alloc("ck", [128, 2, CH, 384]) for _ in range(2)]; vqs = [alloc("vq", [128, 2, CH, IL]) for _ in range(2)]; ysb = [alloc("ysb", [128, 2, CH, IL]) for _ in range(2)]
        V("dve", LZ("memset", Sst[:], 0.0), [], [Sst])
        pstep = arena_t[:, 0:8].ap[0][0]
        for ci in range(T // CH):
            n0 = ci * CH; ck = cks[ci % 2]; vq = vqs[ci % 2]; yb = ysb[ci % 2]
            bases = [n0, (256 - n0 - CH) if n0 < 256 else (2560 - n0 - CH)]
            for d in range(2):
                for iq in range(IQ):
                    S.dma("sp", ck[iq * NBH:(iq + 1) * NBH, d, :, :], bass.AP(rec.tensor, (d * T + bases[d]) * NB * H * 384, [[384, NBH], [NB * H * 384, CH], [1, 384]]), writes=[ck])
                    S.dma("sp", vq[iq * NBH:(iq + 1) * NBH, d, :, :], bass.AP(rec.tensor, (d * T + bases[d]) * NB * H * 384 + 320 + iq * IL, [[384, NBH], [NB * H * 384, CH], [1, IL]]), writes=[vq])
            cko = ck.t.offset; vqo = vq.t.offset; ybo = yb.t.offset
            for s_ in range(CH):
                dsr = (CH + CH - 1 - 2 * s_)

                def fld(fi, s_=s_, dsr=dsr, cko=cko):
                    return bass.AP(arena_t, cko + s_ * 384 + fi * 64, [[pstep, 128], [dsr * 384, 2], [0, IL], [1, 64]])
                v_ap = bass.AP(arena_t, vqo + s_ * IL, [[pstep, 128], [dsr * IL, 2], [1, IL], [0, 64]])
                y_ap = bass.AP(arena_t, ybo + s_ * IL, [[pstep, 128], [dsr * IL, 2], [1, IL]])
                a_ap, r_ap, w_ap, k_ap, b_ap = fld(0), fld(1), fld(2), fld(3), fld(4)
                V("pool", LZ("tensor_tensor", out=tmp2[:], in0=v_ap, in1=k_ap, op=ALU.mult), [vq, ck], [tmp2])
                V("dve", LZ("tensor_tensor", out=tmp[:], in0=Sst[:], in1=a_ap, op=ALU.mult), [Sst, ck], [tmp])
                V("dve", LZ("tensor_reduce", out=sa[:], in_=tmp[:], axis=AX.X, op=ALU.add), [tmp], [sa])
                V("dve", LZ("tensor_tensor", out=Sst[:], in0=Sst[:], in1=w_ap, op=ALU.mult), [Sst, ck], [Sst])
                V("dve", LZ("tensor_tensor", out=tmp[:], in0=sa[:].unsqueeze(3).to_broadcast([128, 2, IL, 64]), in1=b_ap, op=ALU.mult), [sa, ck], [tmp])
                V("dve", LZ("tensor_tensor", out=Sst[:], in0=Sst[:], in1=tmp[:], op=ALU.add), [Sst, tmp], [Sst])
                V("dve", LZ("tensor_tensor", out=Sst[:], in0=Sst[:], in1=tmp2[:], op=ALU.add), [Sst, tmp2], [Sst])
                V("dve", LZ("tensor_tensor", out=tmp[:], in0=Sst[:], in1=r_ap, op=ALU.mult), [Sst, ck], [tmp])
                V("dve", LZ("tensor_reduce", out=y_ap, in_=tmp[:], axis=AX.X, op=ALU.add), [tmp], [yb])
            for d in range(2):
                for iq in range(IQ):
                    S.dma("sp", bass.AP(ysd.tensor, (d * T + bases[d]) * NB * H * 64 + iq * IL, [[64, NBH], [NB * H * 64, CH], [1, IL]]), yb[iq * NBH:(iq + 1) * NBH, d, :, :], reads=[yb])
        phase_end()

        lw = alloc("lw", [128, 512]); lb = alloc("lb", [128, 512])
        S.dma("sp", lw[:], bass.AP(lnx_w.tensor, l * R, [[0, 128], [1, 512]]), writes=[lw]); S.dma("sp", lb[:], bass.AP(lnx_b.tensor, l * R, [[0, 128], [1, 512]]), writes=[lb])
        y0s = [alloc("y0", [128, 8, 64]) for _ in range(2)]; y1s = [alloc("y1", [128, 8, 64]) for _ in range(2)]; bgl = [alloc("bgl", [128, 2, 512]) for _ in range(2)]
        yy = alloc("yy", [128, 8, 64]); s1 = alloc("s1", [128, 8]); s2 = alloc("s2", [128, 8]); ysq = alloc("ysq", [128, 8, 64]); rwb = alloc("rwb", [128, 512], BF16)
        catr = [alloc("catr", [128, 4, 256], BF16) for _ in range(2)]
        ie = 0
        for b in range(NB):
            for g in range(G):
                pos0 = (0 if g == 0 else 256 + (g - 1) * 256); ct = catr[g % 2]
                for tt in range(2):
                    y0 = y0s[ie % 2]; y1 = y1s[ie % 2]; bg = bgl[ie % 2]; ie += 1
                    pos = pos0 + tt * 128
                    S.dma("sp", y0[:].rearrange("p h j -> p (h j)"), bass.AP(ysd.tensor, ((0 * T + pos) * NB + b) * 512, [[NB * 512, 128], [1, 512]]), writes=[y0])
                    S.dma("sp", y1[:].rearrange("p h j -> p (h j)"), bass.AP(ysd.tensor, ((1 * T + pos) * NB + b) * 512, [[NB * 512, 128], [1, 512]]), writes=[y1])
                    S.dma("sp", bg[:].rearrange("p a c -> p (a c)"), bass.AP(bgs.tensor, (b * T + pos) * 1024, [[1024, 128], [1, 1024]]), writes=[bg])
                    V("dve", LZ("tensor_tensor", out=yy[:], in0=y0[:], in1=y1[:], op=ALU.add), [y0, y1], [yy])
                    V("dve", LZ("tensor_reduce", out=s1[:], in_=yy[:], axis=AX.X, op=ALU.add), [yy], [s1])
                    V("dve", LZ("tensor_scalar", out=s1[:], in0=s1[:], scalar1=-1.0 / 64, scalar2=None, op0=ALU.mult), [s1], [s1])
                    V("dve", LZ("tensor_tensor", out=yy[:], in0=yy[:], in1=s1[:].unsqueeze(2).to_broadcast([128, 8, 64]), op=ALU.add), [yy, s1], [yy])
                    V("pool", LZ("tensor_tensor", out=ysq[:], in0=yy[:], in1=yy[:], op=ALU.mult), [yy], [ysq])
                    V("dve", LZ("tensor_reduce", out=s2[:], in_=ysq[:], axis=AX.X, op=ALU.add), [ysq], [s2])
                    V("act", LZ("activation", out=s2[:], in_=s2[:], func=AF.Sqrt, bias=epsl[:], scale=1.0 / 64), [s2, epsl], [s2])
                    V("dve", LZ("reciprocal", out=s2[:], in_=s2[:]), [s2], [s2])
                    V("dve", LZ("tensor_tensor", out=yy[:], in0=yy[:], in1=s2[:].unsqueeze(2).to_broadcast([128, 8, 64]), op=ALU.mult), [yy, s2], [yy])
                    yf = yy[:].rearrange("p h j -> p (h j)")
                    V("pool", LZ("tensor_tensor", out=yf, in0=yf, in1=lw[:], op=ALU.mult), [yy, lw], [yy])
                    V("pool", LZ("tensor_tensor", out=yf, in0=yf, in1=lb[:], op=ALU.add), [yy, lb], [yy])
                    V("dve", LZ("tensor_tensor", out=yf, in0=yf, in1=bg[:, 0, :], op=ALU.add), [yy, bg], [yy])
                    V("dve", LZ("tensor_tensor", out=rwb[:], in0=yf, in1=bg[:, 1, :], op=ALU.mult), [yy, bg], [rwb])
                    pb = bank(); pbv = pb[:].bitcast(BF16)
                    for cc in range(4):
                        V("pe", LZ("transpose", out=pbv[:, cc * 128:(cc + 1) * 128], in_=rwb[:, cc * 128:(cc + 1) * 128], identity=identb[:]), [rwb, identb], [pb])
                    V("act", LZ("copy", out=ct[:, :, tt * 128:(tt + 1) * 128], in_=pbv[:, 0:512].rearrange("p (a b) -> p a b", a=4, b=128)), [pb], [ct])
                S.dma("pool", bass.AP(cats.tensor, (b * G + g) * 128 * 2048 + 1024, [[2048, 128], [256, 4], [1, 256]]), ct[:], reads=[ct])
        phase_end()

        def load_w(Wt, src_t, l_off, rows_k, ncols, stg_):
            for k in range(rows_k):
                for c0 in range(0, ncols, 2048):
                    cw = min(2048, ncols - c0)
                    S.dma("sp", stg_[:, 0:cw], bass.AP(src_t.tensor, l_off + k * 128 * ncols + c0, [[ncols, 128], [1, cw]]), writes=[stg_])
                    if k % 2:
                        V("act", LZ("copy", out=Wt[:, k, c0:c0 + cw], in_=stg_[:, 0:cw]), [stg_], [Wt])
                    else:
                        V("dve", LZ("tensor_copy", out=Wt[:, k, c0:c0 + cw], in_=stg_[:, 0:cw]), [stg_], [Wt])
        Wo = alloc("Wo", [128, 8, 1024], BF16); W1 = alloc("W1", [128, 8, 4096], BF16); stg = alloc("stg", [128, 2048])
        load_w(Wo, w_out, l * D * D, 8, 1024, stg); load_w(W1, w_mlp1, l * D * DFF, 8, 4096, stg)
        cin = [alloc("cin", [128, 8, 256], BF16) for _ in range(2)]; xin = [alloc("xin", [128, 8, 256]) for _ in range(2)]
        x1 = alloc("x1", [128, 8, 256]); sq = alloc("sq", [128, 8, 256], BF16); rstd = alloc("rstd", [128, 256]); tmpa = alloc("tmpa", [128, 256])
        tn = alloc("tn", [128, 8, 256]); h2 = alloc("h2", [128, 8, 256], BF16); rl = [alloc("rl", [128, 256]) for _ in range(2)]; ut = alloc("ut", [128, 32, 256], BF16)
        for b in range(NB):
            for g in range(G):
                ci_ = cin[g % 2]; xa = xin[g % 2]; row = NB if g == 0 else b
                S.dma("sp", ci_[:], cats[b, g].rearrange("p (a b) -> p a b", a=8, b=256), writes=[ci_])
                S.dma("sp", xa[:], xT[b, g].rearrange("p (a b) -> p a b", a=8, b=256), writes=[xa])
                for m in range(8):
                    pb = bank()
                    for k in range(8):
                        V("pe", LZ("matmul", pb[:, 0:256], lhsT=Wo[:, k, m * 128:(m + 1) * 128], rhs=ci_[:, k, :], start=(k == 0), stop=(k == 7)), [Wo, ci_], [pb])
                    V("dve", LZ("scalar_tensor_tensor", out=x1[:, m, :], in0=pb[:, 0:256], scalar=MODS[l][:, 16 + m, row:row + 1], in1=xa[:, m, :], op0=ALU.mult, op1=ALU.add), [pb, MODS[l], xa], [x1])
                S.dma("pool", xT[b, g].rearrange("p (a b) -> p a b", a=8, b=256), x1[:], reads=[x1])
                rms_fm((x1, x1[:]), 8, 256, 1.0 / D, sq, rstd, tmpa)
                V("dve", LZ("tensor_tensor", out=tn[:], in0=x1[:], in1=rstd[:].unsqueeze(1).to_broadcast([128, 8, 256]), op=ALU.mult), [x1, rstd], [tn])
                for k in range(8):
                    if k % 2:
                        V("act", LZ("activation", out=h2[:, k, :], in_=tn[:, k, :], func=AF.Identity, bias=MODS[l][:, 24 + k, row:row + 1], scale=COEF2[l][:, k, row:row + 1]), [tn, MODS[l], COEF2[l]], [h2])
                    else:
                        V("pool", LZ("tensor_scalar", out=h2[:, k, :], in0=tn[:, k, :], scalar1=COEF2[l][:, k, row:row + 1], scalar2=MODS[l][:, 24 + k, row:row + 1], op0=ALU.mult, op1=ALU.add), [tn, MODS[l], COEF2[l]], [h2])
                for m in range(32):
                    pb = bank(); r_ = rl[m % 2]
                    for k in range(8):
                        V("pe", LZ("matmul", pb[:, 0:256], lhsT=W1[:, k, m * 128:(m + 1) * 128], rhs=h2[:, k, :], start=(k == 0), stop=(k == 7)), [W1, h2], [pb])
                    V("act", LZ("activation", out=r_[:], in_=pb[:, 0:256], func=AF.Relu), [pb], [r_])
                    V("pool" if m % 2 else "dve", LZ("tensor_tensor", out=ut[:, m, :], in0=r_[:], in1=r_[:], op=ALU.mult), [r_], [ut])
                S.dma("pool", us[b, g].rearrange("p (a b) -> p a b", a=32, b=256), ut[:], reads=[ut])
        phase_end()

        W2 = alloc("W2", [128, 32, 1024], BF16); stg = alloc("stg", [128, 2048])
        load_w(W2, w_mlp2, l * DFF * D, 32, 1024, stg)
        uin = [alloc("uin", [128, 32, 256], BF16) for _ in range(2)]; xin = [alloc("xin", [128, 8, 256]) for _ in range(2)]; x2 = [alloc("x2", [128, 8, 256]) for _ in range(2)]
        for b in range(NB):
            for g in range(G):
                u_ = uin[g % 2]; xa = xin[g % 2]; xo = x2[g % 2]; row = NB if g == 0 else b
                S.dma("sp", u_[:], us[b, g].rearrange("p (a b) -> p a b", a=32, b=256), writes=[u_])
                S.dma("sp", xa[:], xT[b, g].rearrange("p (a b) -> p a b", a=8, b=256), writes=[xa])
                for m in range(8):
                    pb = bank()
                    for k in range(32):
                        V("pe", LZ("matmul", pb[:, 0:256], lhsT=W2[:, k, m * 128:(m + 1) * 128], rhs=u_[:, k, :], start=(k == 0), stop=(k == 31)), [W2, u_], [pb])
                    V("dve", LZ("scalar_tensor_tensor", out=xo[:, m, :], in0=pb[:, 0:256], scalar=MODS[l][:, 40 + m, row:row + 1], in1=xa[:, m, :], op0=ALU.mult, op1=ALU.add), [pb, MODS[l], xa], [xo])
                S.dma("pool", xT[b, g].rearrange("p (a b) -> p a b", a=8, b=256), xo[:], reads=[xo])
        phase_end()

    if last:
        fg = alloc("fg", [128, 8]); fm_vec(fg, fg[:], final_g, 0, 8)
        xin = [alloc("xin", [128, 8, 256]) for _ in range(2)]; sq = alloc("sq", [128, 8, 256], BF16); rstd = alloc("rstd", [128, 256]); tmpa = alloc("tmpa", [128, 256])
        tn = alloc("tn", [128, 8, 256]); yo = [alloc("yo", [128, 1024]) for _ in range(2)]
        io = 0
        for b in range(NB):
            for g in range(1, G):
                xa = xin[g % 2]
                S.dma("sp", xa[:], xT[b, g].rearrange("p (a b) -> p a b", a=8, b=256), writes=[xa])
                rms_fm((xa, xa[:]), 8, 256, 1.0 / D, sq, rstd, tmpa)
                V("dve", LZ("tensor_tensor", out=tn[:], in0=xa[:], in1=rstd[:].unsqueeze(1).to_broadcast([128, 8, 256]), op=ALU.mult), [xa, rstd], [tn])
                V("pool", LZ("tensor_tensor", out=tn[:], in0=tn[:], in1=fg[:].unsqueeze(2).to_broadcast([128, 8, 256]), op=ALU.mult), [tn, fg], [tn])
                for tt in range(2):
                    yt = yo[io % 2]; io += 1
                    for hf in range(2):
                        pb = bank()
                        for j in range(4):
                            V("pe", LZ("transpose", out=pb[:, j * 128:(j + 1) * 128], in_=tn[:, hf * 4 + j, tt * 128:(tt + 1) * 128], identity=ident[:]), [tn, ident], [pb])
                        if hf:
                            V("act", LZ("copy", out=yt[:, hf * 512:(hf + 1) * 512], in_=pb[:, 0:512]), [pb], [yt])
                        else:
                            V("dve", LZ("tensor_copy", out=yt[:, hf * 512:(hf + 1) * 512], in_=pb[:, 0:512]), [pb], [yt])
                    r0 = (g - 1) * 256 + tt * 128
                    out_dmas.append(S.dma("sp", y_out[b, r0:r0 + 128, :], yt[:], reads=[yt]))
    stats = S.emit(final_waits=out_dmas)
    S.es.close()
    return nc, stats


def _tables():
    pos = np.arange(SEQ)
    row = (pos // 64).astype(np.float32); col = (pos % 64).astype(np.float32)
    inv = (10000.0 ** (-np.arange(8, dtype=np.float32) / 8)).astype(np.float32)
    ar = row[:, None] * inv; ac = col[:, None] * inv
    ang = np.concatenate([ar, ar, ac, ac], axis=-1).astype(np.float32)
    cos_t = np.ones((96, T), np.float32); sin_t = np.zeros((96, T), np.float32)
    cos_t[64:96, TC:] = np.cos(ang).T; sin_t[64:96, TC:] = np.sin(ang).T
    return cos_t, sin_t


_CACHE = {}
_LAYER_W = ["ada_w", "ada_b", "norm1_g", "norm2_g", "w_in", "q_norm_g", "w_uq", "kv_norm_g", "w_ukv", "shift_mu", "decay_w0", "decay_w2",
            "iclr_a0", "iclr_a2", "gate_g2", "k_k", "k_a", "r_k", "lnx_w", "lnx_b", "w_out", "w_mlp1", "w_mlp2"]
_VRES_W = ["vres_v1", "vres_v2", "vres_v0"]


def _f(a):
    return np.ascontiguousarray(np.asarray(a, dtype=np.float32))


def run(inputs, NB, L, ncores):
    key = (NB, L)
    if key not in _CACHE:
        _CACHE[key] = build(NB, L)
    nc, stats = _CACHE[key]
    cos_t, sin_t = _tables()
    shared = {k: _f(v) for k, v in inputs.items() if k not in ("x", "c", "ctx", "c_ctx", "final_g", "r_k")}
    shared["c_ctx"] = _f(inputs["c_ctx"]).reshape(1, D); shared["final_g"] = _f(inputs["final_g"]).reshape(1, D)
    shared["r_k"] = _f(inputs["r_k"]).reshape(4, R)
    shared["ident"] = np.eye(128, dtype=np.float32); shared["cos_t"] = cos_t; shared["sin_t"] = sin_t
    in_maps = []
    for i in range(ncores):
        m = dict(shared)
        m["x"] = _f(inputs["x"][i * NB:(i + 1) * NB]); m["c"] = _f(inputs["c"][i * NB:(i + 1) * NB]); m["ctx"] = _f(inputs["ctx"][i * NB:(i + 1) * NB])
        in_maps.append(m)
    res = run_bass_kernel_spmd(nc, in_maps, core_ids=list(range(ncores)))
    return np.concatenate([r["y"] for r in res.results], axis=0)


def run_layers(inputs, NB, L, ncores):
    cos_t, sin_t = _tables()
    state = [None] * ncores
    out = None
    for l in range(L):
        first = l == 0; last = l == L - 1; has_vres = l > 0
        key = ("layer", NB, has_vres, first, last)
        if key not in _CACHE:
            _CACHE[key] = build(NB, 1, mode=(has_vres, first, last))
        nc, stats = _CACHE[key]
        shared = {}
        for k in _LAYER_W:
            a = _f(inputs[k])
            if k == "r_k":
                a = a.reshape(a.shape[0], R)
            shared[k] = np.ascontiguousarray(a[l:l + 1])
        vl = max(l - 1, 0)
        for k in _VRES_W:
            shared[k] = np.ascontiguousarray(_f(inputs[k])[vl:vl + 1])
        shared["c_ctx"] = _f(inputs["c_ctx"]).reshape(1, D); shared["final_g"] = _f(inputs["final_g"]).reshape(1, D)
        shared["ident"] = np.eye(128, dtype=np.float32); shared["cos_t"] = cos_t; shared["sin_t"] = sin_t
        in_maps = []
        for i in range(ncores):
            m = dict(shared)
            m["c"] = _f(inputs["c"][i * NB:(i + 1) * NB])
            if first:
                m["x"] = _f(inputs["x"][i * NB:(i + 1) * NB]); m["ctx"] = _f(inputs["ctx"][i * NB:(i + 1) * NB])
            else:
                m["xT_i"] = state[i][0]; m["vfs_i"] = state[i][1]
            in_maps.append(m)
        res = run_bass_kernel_spmd(nc, in_maps, core_ids=list(range(ncores)))
        if last:
            out = np.concatenate([r["y"] for r in res.results], axis=0)
        else:
            state = [(r["xT_o"], r["vfs_o"]) for r in res.results]
    return out


def kernel(**inputs):
    return run_layers(inputs, 4, 4, 8).astype(np.float32)
```

```python
import contextlib, math
import numpy as np
import concourse.bass as bass
import concourse.mybir as mybir
from concourse.bass_utils import run_bass_kernel_spmd

F32 = mybir.dt.float32
BF16 = mybir.dt.bfloat16
ALU = mybir.AluOpType
AF = mybir.ActivationFunctionType
AX = mybir.AxisListType
ENGS = ("pe", "dve", "act", "pool", "sp")


def LZ(name, *args, **kwargs):
    return lambda e: getattr(e, name)(*args, **kwargs)


class Buf:
    __slots__ = ("name", "t", "ws", "rd")

    def __init__(self, name, t=None):
        self.name = name; self.t = t; self.ws = []; self.rd = []

    def __getitem__(self, k):
        return self.t[k]


class Op:
    __slots__ = ("eng", "fn", "deps", "is_dma", "idx", "needed", "semval", "sem", "prev_same_sem")

    def __init__(self, eng, fn, is_dma):
        self.eng = eng; self.fn = fn; self.deps = []; self.is_dma = is_dma
        self.needed = False; self.semval = None; self.sem = None; self.prev_same_sem = None


class Sched:
    def __init__(self, nc, n_dma_sems=32):
        self.nc = nc
        self.ops = {e: [] for e in ENGS}
        self.all_ops = []
        self.n_dma_sems = n_dma_sems
        self.es = contextlib.ExitStack()
        self.dma_since_barrier = []

    def sb(self, name, shape, dtype):
        return Buf(name, self.es.enter_context(self.nc.sbuf_tensor(name, list(shape), dtype)))

    def ps(self, name, shape, dtype=F32):
        return Buf(name, self.es.enter_context(self.nc.psum_tensor(name, list(shape), dtype)))

    def op(self, eng, fn, reads=(), writes=(), is_dma=False):
        o = Op(eng, fn, is_dma)
        deps = []
        for b in reads:
            deps.extend(b.ws)
        for b in writes:
            for w in b.ws:
                if w.is_dma and is_dma:
                    continue
                if (not w.is_dma) and (not is_dma) and w.eng == eng:
                    continue
                deps.append(w)
            for r_ in b.rd:
                if (not r_.is_dma) and (not is_dma) and r_.eng == eng:
                    continue
                deps.append(r_)
        seen = set()
        for d in deps:
            if id(d) in seen:
                continue
            seen.add(id(d)); o.deps.append(d)
        for b in reads:
            b.rd.append(o)
        for b in writes:
            if b.rd:
                b.ws = [o]; b.rd = []
            else:
                b.ws.append(o)
                if len(b.ws) > 64:
                    b.ws = b.ws[-64:]
        o.idx = len(self.all_ops)
        self.all_ops.append(o); self.ops[eng].append(o)
        if is_dma:
            self.dma_since_barrier.append(o)
        return o

    def dma(self, eng, out_ap, in_ap, reads=(), writes=(), **kw):
        return self.op(eng, LZ("dma_start", out=out_ap, in_=in_ap, **kw), reads, writes, is_dma=True)

    def barrier(self):
        last = [self.ops[e][-1] for e in ENGS if self.ops[e]]
        dmas = list(reversed(self.dma_since_barrier))
        self.dma_since_barrier = []
        for e in ENGS:
            o = Op(e, None, False)
            o.deps = [d for d in last if d.eng != e and not d.is_dma] + dmas
            o.idx = len(self.all_ops)
            self.all_ops.append(o); self.ops[e].append(o)

    def emit(self, final_waits=()):
        nc = self.nc
        for o in self.all_ops:
            best = {}; nd = []
            for d in o.deps:
                if d.is_dma:
                    nd.append(d)
                elif d.eng not in best or d.idx > best[d.eng].idx:
                    best[d.eng] = d
            o.deps = nd + list(best.values())
            for d in o.deps:
                d.needed = True
        for o in final_waits:
            o.needed = True
        nsem = [0]

        def new_sem(e):
            nsem[0] += 1
            return self.es.enter_context(nc.semaphore("s_%s_%d" % (e, nsem[0])))
        eng_sems = {e: new_sem(e) for e in ENGS}
        dma_sems = [self.es.enter_context(nc.semaphore("d%d" % i)) for i in range(self.n_dma_sems)]
        cnt = {e: 0 for e in ENGS}
        dcnt = [0] * self.n_dma_sems
        dlast = [None] * self.n_dma_sems
        k = 0
        for o in self.all_ops:
            if o.fn is None:
                if cnt[o.eng] > 12000:
                    eng_sems[o.eng] = new_sem(o.eng); cnt[o.eng] = 0
                continue
            if o.is_dma:
                s = k % self.n_dma_sems; k += 1
                dcnt[s] += 16
                o.sem = dma_sems[s]; o.semval = dcnt[s]; o.prev_same_sem = dlast[s]; dlast[s] = o
            elif o.needed:
                cnt[o.eng] += 1
                o.sem = eng_sems[o.eng]; o.semval = cnt[o.eng]
        nw = [0]; nwe = {}; nwd = {}
        with nc.Block() as block:
            def make(engname):
                def body(e):
                    known = {}
                    for o in self.ops[engname]:
                        deps = o.deps
                        if o.is_dma and o.prev_same_sem is not None:
                            deps = deps + [o.prev_same_sem]
                        mx = {}
                        for d in deps:
                            if d.sem is None:
                                continue
                            key = id(d.sem)
                            if key not in mx or d.semval > mx[key].semval:
                                mx[key] = d
                        for key, d in mx.items():
                            if known.get(key, 0) >= d.semval:
                                continue
                            e.wait_ge(d.sem, d.semval); nw[0] += 1; nwe[engname] = nwe.get(engname, 0) + 1; nwd[(engname, d.eng, d.is_dma)] = nwd.get((engname, d.eng, d.is_dma), 0) + 1
                            known[key] = d.semval
                        if o.fn is None:
                            continue
                        ins = o.fn(e)
                        if o.sem is not None:
                            ins.then_inc(o.sem, 16 if o.is_dma else 1)
                    if engname == "sp":
                        for o in final_waits:
                            if known.get(id(o.sem), 0) < o.semval:
                                e.wait_ge(o.sem, o.semval); known[id(o.sem)] = o.semval
                return body
            block.tensor(make("pe")); block.vector(make("dve")); block.scalar(make("act"))
            block.gpsimd(make("pool")); block.sync(make("sp"))
        return dict(n_ops={e: len(v) for e, v in self.ops.items()}, n_waits=nw[0], nwe=nwe, nwd=nwd)


D = 1024; SEQ = 2048; TC = 256; T = SEQ + TC; G = T // 256; H = 8; R = 512
CIN = 2336; MLA_IN = 416; DFF = 4096
NORM_EPS = 1e-6; LNX_EPS = 64e-5
MLA_SCALE = 96 ** -0.5
WP = 2308


def gcol(g):
    return 1 if g == 0 else 259 + (g - 1) * 256


def build(NB, L, CH=16, mode=None):
    nc = bass.Bass("TRN2", target_bir_lowering=False)
    fused = mode is None
    has_vres, first, last = (False, True, True) if fused else mode
    NLW = 4 if fused else 1; NLV = 3 if fused else 1
    IQ = 128 // (NB * 8); IL = 64 // IQ; RW = NB + 1
    S = Sched(nc)
    di = {}

    def inp(name, shape, dt=F32):
        di[name] = nc.dram_tensor(name, list(shape), dt, kind="ExternalInput").ap(); return di[name]

    def scr(name, shape, dt=F32):
        return nc.dram_tensor(name, list(shape), dt, kind="Internal").ap()

    c = inp("c", [NB, D]); c_ctx = inp("c_ctx", [1, D])
    if first:
        x = inp("x", [NB, SEQ, D]); ctx = inp("ctx", [NB, TC, D])
    ada_w = inp("ada_w", [NLW, D, 6 * D]); ada_b = inp("ada_b", [NLW, 6 * D])
    norm1_g = inp("norm1_g", [NLW, D]); norm2_g = inp("norm2_g", [NLW, D]); w_in = inp("w_in", [NLW, D, CIN])
    q_norm_g = inp("q_norm_g", [NLW, 256]); w_uq = inp("w_uq", [NLW, 256, 768]); kv_norm_g = inp("kv_norm_g", [NLW, 128])
    w_ukv = inp("w_ukv", [NLW, 128, 1024]); shift_mu = inp("shift_mu", [NLW, 1920])
    decay_w0 = inp("decay_w0", [NLW, 2, R]); decay_w2 = inp("decay_w2", [NLW, 2, 64, R])
    iclr_a0 = inp("iclr_a0", [NLW, 2, R]); iclr_a2 = inp("iclr_a2", [NLW, 2, 64, R]); gate_g2 = inp("gate_g2", [NLW, 128, R])
    k_k = inp("k_k", [NLW, R]); k_a = inp("k_a", [NLW, R]); r_k = inp("r_k", [NLW, R])
    lnx_w = inp("lnx_w", [NLW, R]); lnx_b = inp("lnx_b", [NLW, R])
    vres_v1 = inp("vres_v1", [NLV, D, 32]); vres_v2 = inp("vres_v2", [NLV, 32, R]); vres_v0 = inp("vres_v0", [NLV, R])
    w_out = inp("w_out", [NLW, D, D]); w_mlp1 = inp("w_mlp1", [NLW, D, DFF]); w_mlp2 = inp("w_mlp2", [NLW, DFF, D])
    final_g = inp("final_g", [1, D])
    ident_d = inp("ident", [128, 128]); cos_d = inp("cos_t", [96, T]); sin_d = inp("sin_t", [96, T])
    if last:
        y_out = nc.dram_tensor("y", [NB, SEQ, D], F32, kind="ExternalOutput").ap()

    if last:
        xT = scr("xT", [NB, G, 128, 8 * 256]); vfs = scr("vfs", [NB, G, 128, 4 * 256])
    else:
        xT = nc.dram_tensor("xT_o", [NB, G, 128, 8 * 256], F32, kind="ExternalOutput").ap()
        vfs = nc.dram_tensor("vfs_o", [NB, G, 128, 4 * 256], F32, kind="ExternalOutput").ap()
    if not first:
        xT_i = inp("xT_i", [NB, G, 128, 8 * 256]); vfs_i = inp("vfs_i", [NB, G, 128, 4 * 256])
    hTs = scr("hTs", [NB, 128, 8 * WP], BF16)
    qTs = scr("qTs", [NB, H, 96, T], BF16); kTs = scr("kTs", [NB, H, 96, T], BF16)
    Vs = scr("Vs", [NB, 18, 128, H * 65], BF16)
    rec = scr("rec", [2, T, NB, H, 384]); ysd = scr("ysd", [2, T, NB, H * 64]); bgs = scr("bgs", [NB, T, 1024])
    cats = scr("cats", [NB, G, 128, 8 * 256], BF16)
    us = scr("us", [NB, G, 128, 32 * 256], BF16)

    AW = 44000
    arena_t = S.es.enter_context(nc.sbuf_tensor("arena", [128, AW], F32))
    st = {"top": 0, "n": 0}

    def alloc(name, shape, dt=F32):
        n = int(np.prod(shape[1:])); words = n if dt == F32 else (n + 1) // 2
        words = (words + 7) // 8 * 8
        assert st["top"] + words <= AW, (name, st["top"], words)
        ap = arena_t[0:shape[0], st["top"]:st["top"] + words]
        st["top"] += words; st["n"] += 1
        if dt != F32:
            ap = ap.bitcast(dt)
        ap = ap[:, 0:n]
        if len(shape) == 3:
            ap = ap.rearrange("p (a b) -> p a b", a=shape[1], b=shape[2])
        elif len(shape) == 4:
            ap = ap.rearrange("p (a b c) -> p a b c", a=shape[1], b=shape[2], c=shape[3])
        return Buf("%s_%d" % (name, st["n"]), ap)

    def phase_end():
        S.barrier(); st["top"] = 0

    banks = [S.ps("bank%d" % i, [128, 512], F32) for i in range(8)]
    bst = {"i": 0}

    def bank():
        b = banks[bst["i"] % 8]; bst["i"] += 1; return b

    def V(eng, fn, r, w):
        return S.op(eng, fn, r, w)

    def small_dma(dst, src, w, eng="sp"):
        return S.dma(eng, dst, src, writes=w, allow_slow_non_contiguous=True)

    def fm_vec(dst_buf, dst_ap, tens, off, nchunk):
        small_dma(dst_ap, bass.AP(tens.tensor, off, [[1, 128], [128, nchunk]]), [dst_buf])

    ident = S.sb("ident_sb", [128, 128], F32); identb = S.sb("identb", [128, 128], BF16)
    onesb = S.sb("onesb", [128, 128], BF16); blk = S.sb("blk", [128, 128], BF16)
    epsn = S.sb("epsn", [128, 1], F32); epsl = S.sb("epsl", [128, 1], F32); eps0 = S.sb("eps0", [128, 1], F32)
    S.dma("sp", ident[:], ident_d[:, :], writes=[ident])
    V("act", LZ("copy", out=identb[:], in_=ident[:]), [ident], [identb])
    V("dve", LZ("memset", onesb[:], 1.0), [], [onesb])
    V("dve", LZ("memset", blk[:], 0.0), [], [blk])
    V("dve", LZ("memset", blk[0:64, 0:64], 1.0), [], [blk])
    V("dve", LZ("memset", blk[64:128, 64:128], 1.0), [], [blk])
    V("dve", LZ("memset", epsn[:], NORM_EPS), [], [epsn])
    V("dve", LZ("memset", epsl[:], LNX_EPS), [], [epsl])
    V("dve", LZ("memset", eps0[:], 1e-24), [], [eps0])
    MODS = [S.sb("mods%d" % l, [128, 48, RW], F32) for l in range(L)]
    COEF1 = [S.sb("coef1_%d" % l, [128, 8, RW], F32) for l in range(L)]
    COEF2 = [S.sb("coef2_%d" % l, [128, 8, RW], F32) for l in range(L)]

    scf = alloc("scf", [128, 8, RW]); scb = alloc("scb", [128, 8, RW], BF16)
    for b in range(NB):
        small_dma(scf[:, :, b], bass.AP(c.tensor, b * D, [[1, 128], [128, 8]]), [scf])
    small_dma(scf[:, :, NB], bass.AP(c_ctx.tensor, 0, [[1, 128], [128, 8]]), [scf])
    V("act", LZ("activation", out=scb[:], in_=scf[:], func=AF.Silu), [scf], [scb])
    wst = [alloc("adast", [128, 6144]) for _ in range(2)]
    wbf = [alloc("adabf", [128, 6144], BF16) for _ in range(2)]
    adab = alloc("adab", [128, 48]); g1t = alloc("g1t", [128, 8]); g2t = alloc("g2t", [128, 8]); tmp8 = alloc("tmp8", [128, 8, RW])
    for l in range(L):
        acc = MODS[l]
        for k in range(8):
            ws_, wb_ = wst[k % 2], wbf[k % 2]
            S.dma("sp", ws_[:], ada_w[l, k * 128:(k + 1) * 128, :], writes=[ws_])
            V("act" if k % 2 else "dve", (LZ("copy", out=wb_[:], in_=ws_[:])) if k % 2 else
              (LZ("tensor_copy", out=wb_[:], in_=ws_[:])), [ws_], [wb_])
            pb = bank()
            pv = pb[:, 0:48 * 8].rearrange("p (a b) -> p a b", a=48, b=8)
            for m in range(48):
                V("pe", LZ("matmul", pv[:, m, 0:RW], lhsT=wb_[:, m * 128:(m + 1) * 128],
                                                                      rhs=scb[:, k, :], start=True, stop=True), [wb_, scb], [pb])
            if k == 0:
                V("dve", LZ("tensor_copy", out=acc[:], in_=pv[:, :, 0:RW]), [pb], [acc])
            else:
                V("dve", LZ("tensor_tensor", out=acc[:], in0=acc[:], in1=pv[:, :, 0:RW], op=ALU.add), [pb, acc], [acc])
        fm_vec(adab, adab[:], ada_b, l * 6144, 48)
        fm_vec(g1t, g1t[:], norm1_g, l * D, 8); fm_vec(g2t, g2t[:], norm2_g, l * D, 8)
        V("dve", LZ("tensor_tensor", out=acc[:], in0=acc[:], in1=adab[:].unsqueeze(2).to_broadcast([128, 48, RW]), op=ALU.add), [acc, adab], [acc])
        for (co, gt, lo) in ((COEF1[l], g1t, 8), (COEF2[l], g2t, 32)):
            V("dve", LZ("tensor_scalar", out=tmp8[:], in0=acc[:, lo:lo + 8, :], scalar1=1.0, scalar2=None, op0=ALU.add), [acc], [tmp8])
            V("dve", LZ("tensor_tensor", out=co[:], in0=tmp8[:], in1=gt[:].unsqueeze(2).to_broadcast([128, 8, RW]), op=ALU.mult), [tmp8, gt], [co])
    phase_end()

    zt = alloc("zt", [128, 8, 2], BF16)
    V("dve", LZ("memset", zt[:], 0.0), [], [zt])
    for b in range(NB):
        for (col, n) in ((0, 1), (257, 2), (2307, 1)):
            small_dma(bass.AP(hTs.tensor, b * 128 * 8 * WP + col, [[8 * WP, 128], [WP, 8], [1, n]]), zt[:, :, 0:n], [], eng="pool")
    if first:
        tin = [alloc("tin", [128, D]) for _ in range(2)]
        xtt = [alloc("xtt", [128, 8, 256]) for _ in range(2)]
        it = 0
        for b in range(NB):
            for g in range(G):
                xo = xtt[(b * G + g) % 2]
                for tt in range(2):
                    ti = tin[it % 2]; it += 1
                    src = ctx[b, tt * 128:(tt + 1) * 128, :] if g == 0 else x[b, (g - 1) * 256 + tt * 128:(g - 1) * 256 + (tt + 1) * 128, :]
                    S.dma("sp", ti[:], src, writes=[ti])
                    for hf in range(2):
                        pb = bank()
                        for j in range(4):
                            kk = hf * 4 + j
                            V("pe", LZ("transpose", out=pb[:, j * 128:(j + 1) * 128], in_=ti[:, kk * 128:(kk + 1) * 128], identity=ident[:]), [ti, ident], [pb])
                        V("act" if hf else "dve", (LZ("copy", out=xo[:, hf * 4:hf * 4 + 4, tt * 128:(tt + 1) * 128], in_=pb[:].rearrange("p (a b) -> p a b", a=4, b=128))) if hf else
                          (LZ("tensor_copy", out=xo[:, hf * 4:hf * 4 + 4, tt * 128:(tt + 1) * 128], in_=pb[:].rearrange("p (a b) -> p a b", a=4, b=128))), [pb], [xo])
                S.dma("pool", xT[b, g].rearrange("p (a b) -> p a b", a=8, b=256), xo[:], reads=[xo])
        phase_end()
    else:
        for b in range(NB):
            S.dma("sp", xT[b], xT_i[b]); S.dma("sp", vfs[b], vfs_i[b])
        phase_end()

    def rms_fm(xin, nk, width, inv_n, sq, rstd, tmp):
        xb, xap = xin
        V("act", LZ("activation", out=sq[:, 0:nk, 0:width], in_=xap, func=AF.Square), [xb], [sq])
        pb = bank()
        for k in range(nk):
            V("pe", LZ("matmul", pb[:, 0:width], lhsT=onesb[:], rhs=sq[:, k, 0:width], start=(k == 0), stop=(k == nk - 1)), [sq, onesb], [pb])
        V("act", LZ("activation", out=tmp[:, 0:width], in_=pb[:, 0:width], func=AF.Sqrt, bias=epsn[:], scale=inv_n), [pb, epsn], [tmp])
        V("dve", LZ("reciprocal", out=rstd[:, 0:width], in_=tmp[:, 0:width]), [tmp], [rstd])

    out_dmas = []
    for l in range(L):
        xt2 = [alloc("xa", [128, 8, 256]) for _ in range(2)]
        sq = alloc("sq", [128, 8, 256], BF16); rstd = alloc("rstd", [128, 256]); tmpa = alloc("tmpa", [128, 256])
        tn = alloc("tn", [128, 8, 256]); hb = [alloc("hb", [128, 8, 256], BF16) for _ in range(2)]
        for b in range(NB):
            for g in range(G):
                xa = xt2[g % 2]; ho = hb[g % 2]; row = NB if g == 0 else b
                S.dma("sp", xa[:], xT[b, g].rearrange("p (a b) -> p a b", a=8, b=256), writes=[xa])
                rms_fm((xa, xa[:]), 8, 256, 1.0 / D, sq, rstd, tmpa)
                V("dve", LZ("tensor_tensor", out=tn[:], in0=xa[:], in1=rstd[:].unsqueeze(1).to_broadcast([128, 8, 256]), op=ALU.mult), [xa, rstd], [tn])
                for k in range(8):
                    V("act" if k % 2 else "pool",
                      (LZ("activation", out=ho[:, k, :], in_=tn[:, k, :], func=AF.Identity, bias=MODS[l][:, k, row:row + 1], scale=COEF1[l][:, k, row:row + 1])) if k % 2 else
                      (LZ("tensor_scalar", out=ho[:, k, :], in0=tn[:, k, :], scalar1=COEF1[l][:, k, row:row + 1], scalar2=MODS[l][:, k, row:row + 1], op0=ALU.mult, op1=ALU.add)),
                      [tn, MODS[l], COEF1[l]], [ho])
                S.dma("pool", bass.AP(hTs.tensor, b * 128 * 8 * WP + gcol(g), [[8 * WP, 128], [WP, 8], [1, 256]]), ho[:], reads=[ho])
        phase_end()

        stg = alloc("stg", [128, CIN])
        Wm = alloc("Wm", [128, 8, 384], BF16); Wkr = alloc("Wkr", [128, 8, 96], BF16); Wkrot = alloc("Wkrot", [128, 8, 96], BF16)
        Wuq = alloc("Wuq", [128, 2, 768], BF16); Wuqr = alloc("Wuqr", [128, 2, 8, 96], BF16); Wukv = alloc("Wukv", [128, 1024], BF16)
        gq = alloc("gq", [128, 2]); gkv = alloc("gkv", [128, 1])
        V("dve", LZ("memset", Wkr[:], 0.0), [], [Wkr]); V("dve", LZ("memset", Wkrot[:], 0.0), [], [Wkrot])
        V("dve", LZ("memset", Wuqr[:], 0.0), [], [Wuqr])
        for k in range(8):
            S.dma("sp", stg[:, 0:MLA_IN], w_in[l, k * 128:(k + 1) * 128, 0:MLA_IN], writes=[stg])
            V("act", LZ("copy", out=Wm[:, k, :], in_=stg[:, 0:384]), [stg], [Wm])
            V("dve", LZ("tensor_copy", out=Wkr[:, k, 64:96], in_=stg[:, 384:416]), [stg], [Wkr])
            for (do, so, sg) in ((64, 392, -1.0), (72, 384, 1.0), (80, 408, -1.0), (88, 400, 1.0)):
                V("dve", LZ("tensor_scalar", out=Wkrot[:, k, do:do + 8], in0=stg[:, so:so + 8], scalar1=sg, scalar2=None, op0=ALU.mult), [stg], [Wkrot])
        for k in range(2):
            S.dma("sp", stg[:, 0:768], w_uq[l, k * 128:(k + 1) * 128, :], writes=[stg])
            V("act", LZ("copy", out=Wuq[:, k, :], in_=stg[:, 0:768]), [stg], [Wuq])
            sv = stg[:, 0:768].rearrange("p (h d) -> p h d", h=8, d=96)
            for (do, so, sg) in ((64, 72, -1.0), (72, 64, 1.0), (80, 88, -1.0), (88, 80, 1.0)):
                V("dve", LZ("tensor_scalar", out=Wuqr[:, k, :, do:do + 8], in0=sv[:, :, so:so + 8], scalar1=sg, scalar2=None, op0=ALU.mult), [stg], [Wuqr])
        S.dma("sp", stg[:, 0:1024], w_ukv[l, :, :], writes=[stg])
        V("act", LZ("copy", out=Wukv[:], in_=stg[:, 0:1024]), [stg], [Wukv])
        fm_vec(gq, gq[:], q_norm_g, l * 256, 2); fm_vec(gkv, gkv[:], kv_norm_g, l * 128, 1)
        hin = [alloc("hin", [128, 8, 256], BF16) for _ in range(2)]
        cost = alloc("cost", [96, 256]); sint = alloc("sint", [96, 256])
        cqs = alloc("cqs", [128, 2, 256]); sq2 = alloc("sq2", [128, 2, 256], BF16); rs2 = alloc("rs2", [128, 256]); tm2 = alloc("tm2", [128, 256])
        cqn = alloc("cqn", [128, 2, 256], BF16); ckvn = alloc("ckvn", [128, 256], BF16); ckvs = alloc("ckvs", [128, 256])
        qo = alloc("qo", [96, 8, 256], BF16); ko = alloc("ko", [96, 8, 256], BF16); vo = alloc("vo", [128, 2, H * 65], BF16)
        t1 = alloc("t1", [96, 256]); t2 = alloc("t2", [96, 256]); krp = alloc("krp", [96, 256], BF16)
        V("dve", LZ("memset", vo[:], 1.0), [], [vo])
        for b in range(NB):
            for g in range(G):
                hi = hin[g % 2]; c0 = (0 if g == 0 else 256 + (g - 1) * 256)
                S.dma("sp", hi[:], bass.AP(hTs.tensor, b * 128 * 8 * WP + gcol(g), [[8 * WP, 128], [WP, 8], [1, 256]]), writes=[hi])
                S.dma("sp", cost[64:96, :], cos_d[64:96, c0:c0 + 256], writes=[cost]); S.dma("sp", sint[64:96, :], sin_d[64:96, c0:c0 + 256], writes=[sint])
                for m in range(3):
                    pb = bank()
                    for k in range(8):
                        V("pe", LZ("matmul", pb[:, 0:256], lhsT=Wm[:, k, m * 128:(m + 1) * 128], rhs=hi[:, k, :], start=(k == 0), stop=(k == 7)), [Wm, hi], [pb])
                    if m < 2:
                        V("dve", LZ("tensor_copy", out=cqs[:, m, :], in_=pb[:, 0:256]), [pb], [cqs])
                    else:
                        V("dve", LZ("tensor_copy", out=ckvs[:], in_=pb[:, 0:256]), [pb], [ckvs])
                rms_fm((cqs, cqs[:]), 2, 256, 1.0 / 256, sq2, rs2, tm2)
                for m in range(2):
                    V("dve", LZ("scalar_tensor_tensor", out=cqn[:, m, :], in0=cqs[:, m, :], scalar=gq[:, m:m + 1], in1=rs2[:], op0=ALU.mult, op1=ALU.mult), [cqs, gq, rs2], [cqn])
                rms_fm((ckvs, ckvs[:].unsqueeze(1)), 1, 256, 1.0 / 128, sq2, rs2, tm2)
                V("dve", LZ("scalar_tensor_tensor", out=ckvn[:], in0=ckvs[:], scalar=gkv[:, 0:1], in1=rs2[:], op0=ALU.mult, op1=ALU.mult), [ckvs, gkv, rs2], [ckvn])
                pk = bank(); pr = bank()
                for k in range(8):
                    V("pe", LZ("matmul", pk[0:96, 0:256], lhsT=Wkr[:, k, :], rhs=hi[:, k, :], start=(k == 0), stop=(k == 7)), [Wkr, hi], [pk])
                for k in range(8):
                    V("pe", LZ("matmul", pr[0:96, 0:256], lhsT=Wkrot[:, k, :], rhs=hi[:, k, :], start=(k == 0), stop=(k == 7)), [Wkrot, hi], [pr])
                V("dve", LZ("tensor_tensor", out=t1[64:96, :], in0=pk[64:96, 0:256], in1=cost[64:96, :], op=ALU.mult), [pk, cost], [t1])
                V("dve", LZ("tensor_tensor", out=t2[64:96, :], in0=pr[64:96, 0:256], in1=sint[64:96, :], op=ALU.mult), [pr, sint], [t2])
                V("dve", LZ("tensor_tensor", out=krp[64:96, :], in0=t1[64:96, :], in1=t2[64:96, :], op=ALU.add), [t1, t2], [krp])
                V("pool", LZ("tensor_copy", out=ko[64:96, :, :], in_=krp[64:96, :].unsqueeze(1).to_broadcast([32, 8, 256])), [krp], [ko])
                for h in range(H):
                    pb = bank()
                    V("pe", LZ("matmul", pb[0:64, 0:256], lhsT=Wukv[:, h * 128:h * 128 + 64], rhs=ckvn[:], start=True, stop=True), [Wukv, ckvn], [pb])
                    V("act", LZ("copy", out=ko[0:64, h, :], in_=pb[0:64, 0:256]), [pb], [ko])
                    pq = bank(); pr2 = bank()
                    for k in range(2):
                        V("pe", LZ("matmul", pq[0:96, 0:256], lhsT=Wuq[:, k, h * 96:(h + 1) * 96], rhs=cqn[:, k, :], start=(k == 0), stop=(k == 1)), [Wuq, cqn], [pq])
                    for k in range(2):
                        V("pe", LZ("matmul", pr2[0:96, 0:256], lhsT=Wuqr[:, k, h, :], rhs=cqn[:, k, :], start=(k == 0), stop=(k == 1)), [Wuqr, cqn], [pr2])
                    V("act", LZ("copy", out=qo[0:64, h, :], in_=pq[0:64, 0:256]), [pq], [qo])
                    V("dve", LZ("tensor_tensor", out=t1[64:96, :], in0=pq[64:96, 0:256], in1=cost[64:96, :], op=ALU.mult), [pq, cost], [t1])
                    V("dve", LZ("tensor_tensor", out=t2[64:96, :], in0=pr2[64:96, 0:256], in1=sint[64:96, :], op=ALU.mult), [pr2, sint], [t2])
                    V("dve", LZ("tensor_tensor", out=qo[64:96, h, :], in0=t1[64:96, :], in1=t2[64:96, :], op=ALU.add), [t1, t2], [qo])
                for tt in range(2):
                    for hf in range(2):
                        pb = bank()
                        V("pe", LZ("matmul", pb[:, 0:512], lhsT=ckvn[:, tt * 128:(tt + 1) * 128], rhs=Wukv[:, hf * 512:(hf + 1) * 512], start=True, stop=True), [ckvn, Wukv], [pb])
                        V("act", LZ("copy", out=vo[:, tt, :].rearrange("p (h d) -> p h d", h=8, d=65)[:, hf * 4:hf * 4 + 4, 0:64],
                                                                         in_=pb[:, 0:512].rearrange("p (h d) -> p h d", h=4, d=128)[:, :, 64:128]), [pb], [vo])
                S.dma("pool", bass.AP(qTs.tensor, b * H * 96 * T + c0, [[T, 96], [96 * T, 8], [1, 256]]), qo[:], reads=[qo])
                S.dma("pool", bass.AP(kTs.tensor, b * H * 96 * T + c0, [[T, 96], [96 * T, 8], [1, 256]]), ko[:], reads=[ko])
                S.dma("pool", bass.AP(Vs.tensor, (b * 18 + 2 * g) * 128 * H * 65, [[H * 65, 128], [128 * H * 65, 2], [1, H * 65]]), vo[:], reads=[vo])
        phase_end()

        Wr = alloc("Wr", [128, 8, 1920], BF16); Wmu = alloc("Wmu", [128, 8, 1920], BF16)
        stg = alloc("stg", [128, 1920]); mub = alloc("mub", [128, 1920])
        S.dma("sp", mub[:], bass.AP(shift_mu.tensor, l * 1920, [[0, 128], [1, 1920]]), writes=[mub])
        for k in range(8):
            S.dma("sp", stg[:], w_in[l, k * 128:(k + 1) * 128, MLA_IN:CIN], writes=[stg])
            V("act", LZ("copy", out=Wr[:, k, :], in_=stg[:]), [stg], [Wr])
            V("dve", LZ("tensor_tensor", out=Wmu[:, k, :], in0=stg[:], in1=mub[:], op=ALU.mult), [stg, mub], [Wmu])
        Wd2 = alloc("Wd2", [128, 512], BF16); Wa2 = alloc("Wa2", [128, 512], BF16); Wg2 = alloc("Wg2", [128, 512], BF16)
        Wv1 = alloc("Wv1", [128, 8, 32], BF16); Wv2 = alloc("Wv2", [32, 512], BF16)
        for (wt, src) in ((Wd2, decay_w2[l].rearrange("d l c -> (d l) c")), (Wa2, iclr_a2[l].rearrange("d l c -> (d l) c")), (Wg2, gate_g2[l])):
            S.dma("sp", stg[:, 0:512], src, writes=[stg])
            V("act", LZ("copy", out=wt[:], in_=stg[:, 0:512]), [stg], [wt])
        vres_on = (l > 0) if fused else has_vres
        vl = (l - 1) if fused else 0
        if vres_on:
            S.dma("sp", stg[:, 0:256].rearrange("p (a b) -> p a b", a=8, b=32), bass.AP(vres_v1.tensor, vl * D * 32, [[32, 128], [128 * 32, 8], [1, 32]]), writes=[stg])
            V("act", LZ("copy", out=Wv1[:], in_=stg[:, 0:256].rearrange("p (a b) -> p a b", a=8, b=32)), [stg], [Wv1])
            S.dma("sp", stg[0:32, 0:512], vres_v2[vl], writes=[stg])
            V("act", LZ("copy", out=Wv2[:], in_=stg[0:32, 0:512]), [stg], [Wv2])
        pv = alloc("pvec", [128, 12, 4])
        for d in range(2):
            fm_vec(pv, pv[:, d, :], decay_w0, (l * 2 + d) * R, 4); fm_vec(pv, pv[:, 2 + d, :], iclr_a0, (l * 2 + d) * R, 4)
        fm_vec(pv, pv[:, 4, :], k_k, l * R, 4); fm_vec(pv, pv[:, 5, :], k_a, l * R, 4); fm_vec(pv, pv[:, 8, :], r_k, l * R, 4)
        if vres_on:
            fm_vec(pv, pv[:, 7, :], vres_v0, vl * R, 4)
        V("dve", LZ("tensor_scalar", out=pv[:, 6, :], in0=pv[:, 5, :], scalar1=-1.0, scalar2=1.0, op0=ALU.mult, op1=ALU.add), [pv], [pv])
        hin = [alloc("hin", [128, 8, 258], BF16) for _ in range(2)]
        dh = alloc("dh", [128, 8, 256], BF16); dht = alloc("dht", [128, 8, 256])
        nm = ["r", "k", "v", "kk", "rn", "sg", "w0", "w1", "a0", "a1", "kd0", "kd1", "b0", "b1", "g", "bon", "t", "vf"]
        Fm = {n: alloc("f_" + n, [128, 256]) for n in nm}
        wlo = alloc("wlo", [128, 256], BF16); alo = alloc("alo", [128, 256], BF16); glo = alloc("glo", [128, 256], BF16)
        vlo = alloc("vlo", [32, 256], BF16); sqk = alloc("sqk", [128, 256], BF16); tb = alloc("tb", [128, 256], BF16)
        recs = [alloc("rect", [128, 2, 384]) for _ in range(4)]; bgts = [alloc("bgt", [128, 2, 128]) for _ in range(2)]
        NEG = -math.exp(-0.5)
        for b in range(NB):
            for g in range(G):
                hi = hin[g % 2]; pos0 = (0 if g == 0 else 256 + (g - 1) * 256)
                S.dma("sp", hi[:], bass.AP(hTs.tensor, b * 128 * 8 * WP + gcol(g) - 1, [[8 * WP, 128], [WP, 8], [1, 258]]), writes=[hi])
                V("pool", LZ("tensor_tensor", out=dht[:], in0=hi[:, :, 0:256], in1=hi[:, :, 2:258], op=ALU.add), [hi], [dht])
                V("dve", LZ("scalar_tensor_tensor", out=dh[:], in0=dht[:], scalar=0.5, in1=hi[:, :, 1:257], op0=ALU.mult, op1=ALU.subtract), [dht, hi], [dh])

                def proj(m, hi=hi):
                    pb = bank()
                    for k in range(8):
                        V("pe", LZ("matmul", pb[:, 0:256], lhsT=Wr[:, k, m * 128:(m + 1) * 128], rhs=hi[:, k, 1:257], start=(k == 0), stop=False), [Wr, hi], [pb])
                    for k in range(8):
                        V("pe", LZ("matmul", pb[:, 0:256], lhsT=Wmu[:, k, m * 128:(m + 1) * 128], rhs=dh[:, k, :], start=False, stop=(k == 7)), [Wmu, dh], [pb])
                    return pb
                pb = proj(12); V("act", LZ("activation", out=wlo[:], in_=pb[:, 0:256], func=AF.Tanh), [pb], [wlo])
                pb = proj(13); V("act", LZ("copy", out=alo[:], in_=pb[:, 0:256]), [pb], [alo])
                pb = proj(14); V("act", LZ("activation", out=glo[:], in_=pb[:, 0:256], func=AF.Sigmoid), [pb], [glo])
                if vres_on:
                    pb = bank()
                    for k in range(8):
                        V("pe", LZ("matmul", pb[0:32, 0:256], lhsT=Wv1[:, k, :], rhs=hi[:, k, 1:257], start=(k == 0), stop=(k == 7)), [Wv1, hi], [pb])
                    V("act", LZ("copy", out=vlo[:], in_=pb[0:32, 0:256]), [pb], [vlo])
                for cc in range(4):
                    f = Fm
                    for (n, m) in (("r", cc), ("k", 4 + cc), ("v", 8 + cc)):
                        pb = proj(m)
                        V("act" if n != "k" else "dve", (LZ("copy", out=f[n][:], in_=pb[:, 0:256])) if n != "k" else
                          (LZ("tensor_copy", out=f[n][:], in_=pb[:, 0:256])), [pb], [f[n]])
                    for d in range(2):
                        pb = bank()
                        V("pe", LZ("matmul", pb[:, 0:256], lhsT=Wd2[64 * d:64 * d + 64, cc * 128:(cc + 1) * 128], rhs=wlo[64 * d:64 * d + 64, :], start=True, stop=True), [Wd2, wlo], [pb])
                        V("act", LZ("activation", out=f["sg"][:], in_=pb[:, 0:256], func=AF.Sigmoid, bias=pv[:, d, cc:cc + 1], scale=1.0), [pb, pv], [f["sg"]])
                        V("act", LZ("activation", out=f["w%d" % d][:], in_=f["sg"][:], func=AF.Exp, scale=NEG), [f["sg"]], [f["w%d" % d]])
                        pb = bank()
                        V("pe", LZ("matmul", pb[:, 0:256], lhsT=Wa2[64 * d:64 * d + 64, cc * 128:(cc + 1) * 128], rhs=alo[64 * d:64 * d + 64, :], start=True, stop=True), [Wa2, alo], [pb])
                        V("act", LZ("activation", out=f["a%d" % d][:], in_=pb[:, 0:256], func=AF.Sigmoid, bias=pv[:, 2 + d, cc:cc + 1], scale=1.0), [pb, pv], [f["a%d" % d]])
                    pb = bank()
                    V("pe", LZ("matmul", pb[:, 0:256], lhsT=Wg2[:, cc * 128:(cc + 1) * 128], rhs=glo[:], start=True, stop=True), [Wg2, glo], [pb])
                    V("act", LZ("copy", out=f["g"][:], in_=pb[:, 0:256]), [pb], [f["g"]])
                    vfd = bass.AP(vfs.tensor, ((b * G + g) * 128) * 1024 + cc * 256, [[1024, 128], [1, 256]])
                    if not vres_on:
                        S.dma("pool", vfd, f["v"][:], reads=[f["v"]])
                    else:
                        S.dma("sp", f["vf"][:], vfd, writes=[f["vf"]])
                        pb = bank()
                        V("pe", LZ("matmul", pb[:, 0:256], lhsT=Wv2[:, cc * 128:(cc + 1) * 128], rhs=vlo[:], start=True, stop=True), [Wv2, vlo], [pb])
                        V("act", LZ("activation", out=f["sg"][:], in_=pb[:, 0:256], func=AF.Sigmoid, bias=pv[:, 7, cc:cc + 1], scale=1.0), [pb, pv], [f["sg"]])
                        V("dve", LZ("tensor_tensor", out=f["t"][:], in0=f["vf"][:], in1=f["v"][:], op=ALU.subtract), [f["vf"], f["v"]], [f["t"]])
                        V("dve", LZ("tensor_tensor", out=f["t"][:], in0=f["t"][:], in1=f["sg"][:], op=ALU.mult), [f["t"], f["sg"]], [f["t"]])
                        V("dve", LZ("tensor_tensor", out=f["v"][:], in0=f["v"][:], in1=f["t"][:], op=ALU.add), [f["v"], f["t"]], [f["v"]])
                    V("dve", LZ("tensor_scalar", out=f["kk"][:], in0=f["k"][:], scalar1=pv[:, 4, cc:cc + 1], scalar2=None, op0=ALU.mult), [f["k"], pv], [f["kk"]])
                    V("act", LZ("activation", out=sqk[:], in_=f["kk"][:], func=AF.Square), [f["kk"]], [sqk])
                    pb = bank()
                    V("pe", LZ("matmul", pb[:, 0:256], lhsT=blk[:], rhs=sqk[:], start=True, stop=True), [blk, sqk], [pb])
                    V("act", LZ("activation", out=f["rn"][:], in_=pb[:, 0:256], func=AF.Sqrt, bias=eps0[:], scale=1.0), [pb, eps0], [f["rn"]])
                    V("dve", LZ("reciprocal", out=f["rn"][:], in_=f["rn"][:]), [f["rn"]], [f["rn"]])
                    V("dve", LZ("tensor_tensor", out=f["kk"][:], in0=f["kk"][:], in1=f["rn"][:], op=ALU.mult), [f["kk"], f["rn"]], [f["kk"]])
                    for d in range(2):
                        V("pool", LZ("tensor_scalar", out=f["t"][:], in0=f["a%d" % d][:], scalar1=pv[:, 5, cc:cc + 1], scalar2=pv[:, 6, cc:cc + 1], op0=ALU.mult, op1=ALU.add), [f["a%d" % d], pv], [f["t"]])
                        V("pool", LZ("tensor_tensor", out=f["kd%d" % d][:], in0=f["k"][:], in1=f["t"][:], op=ALU.mult), [f["k"], f["t"]], [f["kd%d" % d]])
                        V("pool", LZ("tensor_tensor", out=f["b%d" % d][:], in0=f["kk"][:], in1=f["a%d" % d][:], op=ALU.mult), [f["kk"], f["a%d" % d]], [f["b%d" % d]])
                    V("dve", LZ("tensor_scalar", out=f["kk"][:], in0=f["kk"][:], scalar1=-1.0, scalar2=None, op0=ALU.mult), [f["kk"]], [f["kk"]])
                    V("dve", LZ("tensor_tensor", out=f["t"][:], in0=f["kd0"][:], in1=f["kd1"][:], op=ALU.add), [f["kd0"], f["kd1"]], [f["t"]])
                    V("dve", LZ("scalar_tensor_tensor", out=tb[:], in0=f["r"][:], scalar=pv[:, 8, cc:cc + 1], in1=f["t"][:], op0=ALU.mult, op1=ALU.mult), [f["r"], pv, f["t"]], [tb])
                    pb = bank()
                    V("pe", LZ("matmul", pb[:, 0:256], lhsT=blk[:], rhs=tb[:], start=True, stop=True), [blk, tb], [pb])
                    V("dve", LZ("tensor_tensor", out=f["bon"][:], in0=pb[:, 0:256], in1=f["v"][:], op=ALU.mult), [pb, f["v"]], [f["bon"]])
                    for tt in range(2):
                        for d in range(2):
                            fl = ["kk", "r", "w%d" % d, "kd%d" % d, "b%d" % d, "v"]
                            pbs = [bank(), bank()]
                            for fi, n in enumerate(fl):
                                pb = pbs[fi // 3]
                                V("pe", LZ("transpose", out=pb[:, (fi % 3) * 128:(fi % 3 + 1) * 128], in_=f[n][:, tt * 128:(tt + 1) * 128], identity=ident[:]), [f[n], ident], [pb])
                            rt = recs[tt * 2 + d]
                            for hf in range(2):
                                pb = pbs[hf]
                                for hh in range(2):
                                    o_ap = rt[:, hh, hf * 192:hf * 192 + 192].rearrange("p (f j) -> p f j", f=3, j=64)
                                    i_ap = pb[:, 0:384].rearrange("p (f h j) -> p f h j", f=3, h=2, j=64)[:, :, hh, :]
                                    if hh:
                                        V("act", LZ("copy", out=o_ap, in_=i_ap), [pb], [rt])
                                    else:
                                        V("dve", LZ("tensor_copy", out=o_ap, in_=i_ap), [pb], [rt])
                            S.dma("pool", bass.AP(rec.tensor, ((d * T + pos0 + tt * 128) * NB + b) * H * 384 + 2 * cc * 384, [[NB * H * 384, 128], [1, 768]]),
                                  rt[:].rearrange("p h f -> p (h f)"), reads=[rt])
                        pb = bank(); bt = bgts[tt]
                        V("pe", LZ("transpose", out=pb[:, 0:128], in_=f["bon"][:, tt * 128:(tt + 1) * 128], identity=ident[:]), [f["bon"], ident], [pb])
                        V("pe", LZ("transpose", out=pb[:, 128:256], in_=f["g"][:, tt * 128:(tt + 1) * 128], identity=ident[:]), [f["g"], ident], [pb])
                        V("act", LZ("copy", out=bt[:], in_=pb[:, 0:256].rearrange("p (a b) -> p a b", a=2, b=128)), [pb], [bt])
                        S.dma("pool", bass.AP(bgs.tensor, (b * T + pos0 + tt * 128) * 1024 + cc * 128, [[1024, 128], [512, 2], [1, 128]]), bt[:], reads=[bt])
        phase_end()

        banksA = banks[0:4]; accb = banks[4:8]; ai = [0]

        def bankA():
            bb = banksA[ai[0] % 4]; ai[0] += 1; return bb
        Vt = alloc("Vt", [128, 18, H * 65], BF16); att = alloc("att", [128, 18, 512], BF16)
        qh = [alloc("qh", [96, T], BF16) for _ in range(2)]; kh = [alloc("kh", [96, T], BF16) for _ in range(2)]
        pts = [alloc("pt", [128, 512], BF16) for _ in range(3)]; rcp = alloc("rcp", [128, 4]); catt = [alloc("catt", [128, 4, 256], BF16) for _ in range(2)]
        pti = 0
        for b in range(NB):
            S.dma("sp", Vt[:], bass.AP(Vs.tensor, b * 18 * 128 * H * 65, [[H * 65, 128], [128 * H * 65, 18], [1, H * 65]]), writes=[Vt])
            for h in range(H):
                q_ = qh[h % 2]; k_ = kh[h % 2]
                S.dma("sp", q_[:], qTs[b, h], writes=[q_]); S.dma("sp", k_[:], kTs[b, h], writes=[k_])
                for (q0, qw, kts) in [(0, 256, 2)] + [(256 + i * 512, 512, 18) for i in range(4)]:
                    nq = qw // 128
                    for kt in range(kts):
                        pb = bankA(); pt = pts[pti % 3]; pti += 1
                        V("pe", LZ("matmul", pb[:, 0:qw], lhsT=k_[:, kt * 128:(kt + 1) * 128], rhs=q_[:, q0:q0 + qw], start=True, stop=True), [k_, q_], [pb])
                        V("act", LZ("activation", out=pt[:, 0:qw], in_=pb[:, 0:qw], func=AF.Exp, scale=MLA_SCALE), [pb], [pt])
                        for qi in range(nq):
                            V("pe", LZ("matmul", accb[qi][:, 0:65], lhsT=pt[:, qi * 128:(qi + 1) * 128], rhs=Vt[:, kt, h * 65:(h + 1) * 65], start=(kt == 0), stop=(kt == kts - 1)), [pt, Vt], [accb[qi]])
                    for qi in range(nq):
                        qt = q0 // 128 + qi
                        V("dve", LZ("reciprocal", out=rcp[:, qi:qi + 1], in_=accb[qi][:, 64:65]), [accb[qi]], [rcp])
                        V("act", LZ("activation", out=att[:, qt, h * 64:(h + 1) * 64], in_=accb[qi][:, 0:64], func=AF.Copy, scale=rcp[:, qi:qi + 1]), [accb[qi], rcp], [att])
            for g in range(G):
                ct = catt[g % 2]
                for tt in range(2):
                    pb = bankA(); pbv = pb[:].bitcast(BF16)
                    for cc in range(4):
                        V("pe", LZ("transpose", out=pbv[:, cc * 128:(cc + 1) * 128], in_=att[:, 2 * g + tt, cc * 128:(cc + 1) * 128], identity=identb[:]), [att, identb], [pb])
                    V("dve", LZ("tensor_copy", out=ct[:, :, tt * 128:(tt + 1) * 128], in_=pbv[:, 0:512].rearrange("p (a b) -> p a b", a=4, b=128)), [pb], [ct])
                S.dma("pool", bass.AP(cats.tensor, (b * G + g) * 128 * 2048, [[2048, 128], [256, 4], [1, 256]]), ct[:], reads=[ct])
        phase_end()

        NBH = NB * 8
        Sst = alloc("Sst", [128, 2, IL, 64]); tmp = alloc("stmp", [128, 2, IL, 64]); tmp2 = alloc("stmp2", [128, 2, IL, 64]); sa = alloc("sa", [128, 2, IL])
        cks = [alloc("ck", [128, 2, CH, 384]) for _ in range(2)]; vqs = [alloc("vq", [128, 2, CH, IL]) for _ in range(2)]; ysb = [alloc("ysb", [128, 2, CH, IL]) for _ in range(2)]
        V("dve", LZ("memset", Sst[:], 0.0), [], [Sst])
        pstep = arena_t[:, 0:8].ap[0][0]
        for ci in range(T // CH):
            n0 = ci * CH; ck = cks[ci % 2]; vq = vqs[ci % 2]; yb = ysb[ci % 2]
            bases = [n0, (256 - n0 - CH) if n0 < 256 else (2560 - n0 - CH)]
            for d in range(2):
                for iq in range(IQ):
                    S.dma("sp", ck[iq * NBH:(iq + 1) * NBH, d, :, :], bass.AP(rec.tensor, (d * T + bases[d]) * NB * H * 384, [[384, NBH], [NB * H * 384, CH], [1, 384]]), writes=[ck])
                    S.dma("sp", vq[iq * NBH:(iq + 1) * NBH, d, :, :], bass.AP(rec.tensor, (d * T + bases[d]) * NB * H * 384 + 320 + iq * IL, [[384, NBH], [NB * H * 384, CH], [1, IL]]), writes=[vq])
            cko = ck.t.offset; vqo = vq.t.offset; ybo = yb.t.offset
            for s_ in range(CH):
                dsr = (CH + CH - 1 - 2 * s_)

                def fld(fi, s_=s_, dsr=dsr, cko=cko):
                    return bass.AP(arena_t, cko + s_ * 384 + fi * 64, [[pstep, 128], [dsr * 384, 2], [0, IL], [1, 64]])
                v_ap = bass.AP(arena_t, vqo + s_ * IL, [[pstep, 128], [dsr * IL, 2], [1, IL], [0, 64]])
                y_ap = bass.AP(arena_t, ybo + s_ * IL, [[pstep, 128], [dsr * IL, 2], [1, IL]])
                a_ap, r_ap, w_ap, k_ap, b_ap = fld(0), fld(1), fld(2), fld(3), fld(4)
                V("pool", LZ("tensor_tensor", out=tmp2[:], in0=v_ap, in1=k_ap, op=ALU.mult), [vq, ck], [tmp2])
                V("dve", LZ("tensor_tensor", out=tmp[:], in0=Sst[:], in1=a_ap, op=ALU.mult), [Sst, ck], [tmp])
                V("dve", LZ("tensor_reduce", out=sa[:], in_=tmp[:], axis=AX.X, op=ALU.add), [tmp], [sa])
                V("dve", LZ("tensor_tensor", out=Sst[:], in0=Sst[:], in1=w_ap, op=ALU.mult), [Sst, ck], [Sst])
                V("dve", LZ("tensor_tensor", out=tmp[:], in0=sa[:].unsqueeze(3).to_broadcast([128, 2, IL, 64]), in1=b_ap, op=ALU.mult), [sa, ck], [tmp])
                V("dve", LZ("tensor_tensor", out=Sst[:], in0=Sst[:], in1=tmp[:], op=ALU.add), [Sst, tmp], [Sst])
                V("dve", LZ("tensor_tensor", out=Sst[:], in0=Sst[:], in1=tmp2[:], op=ALU.add), [Sst, tmp2], [Sst])
                V("dve", LZ("tensor_tensor", out=tmp[:], in0=Sst[:], in1=r_ap, op=ALU.mult), [Sst, ck], [tmp])
                V("dve", LZ("tensor_reduce", out=y_ap, in_=tmp[:], axis=AX.X, op=ALU.add), [tmp], [yb])
            for d in range(2):
                for iq in range(IQ):
                    S.dma("sp", bass.AP(ysd.tensor, (d * T + bases[d]) * NB * H * 64 + iq * IL, [[64, NBH], [NB * H * 64, CH], [1, IL]]), yb[iq * NBH:(iq + 1) * NBH, d, :, :], reads=[yb])
        phase_end()

        lw = alloc("lw", [128, 512]); lb = alloc("lb", [128, 512])
        S.dma("sp", lw[:], bass.AP(lnx_w.tensor, l * R, [[0, 128], [1, 512]]), writes=[lw]); S.dma("sp", lb[:], bass.AP(lnx_b.tensor, l * R, [[0, 128], [1, 512]]), writes=[lb])
        y0s = [alloc("y0", [128, 8, 64]) for _ in range(2)]; y1s = [alloc("y1", [128, 8, 64]) for _ in range(2)]; bgl = [alloc("bgl", [128, 2, 512]) for _ in range(2)]
        yy = alloc("yy", [128, 8, 64]); s1 = alloc("s1", [128, 8]); s2 = alloc("s2", [128, 8]); ysq = alloc("ysq", [128, 8, 64]); rwb = alloc("rwb", [128, 512], BF16)
        catr = [alloc("catr", [128, 4, 256], BF16) for _ in range(2)]
        ie = 0
        for b in range(NB):
            for g in range(G):
                pos0 = (0 if g == 0 else 256 + (g - 1) * 256); ct = catr[g % 2]
                for tt in range(2):
                    y0 = y0s[ie % 2]; y1 = y1s[ie % 2]; bg = bgl[ie % 2]; ie += 1
                    pos = pos0 + tt * 128
                    S.dma("sp", y0[:].rearrange("p h j -> p (h j)"), bass.AP(ysd.tensor, ((0 * T + pos) * NB + b) * 512, [[NB * 512, 128], [1, 512]]), writes=[y0])
                    S.dma("sp", y1[:].rearrange("p h j -> p (h j)"), bass.AP(ysd.tensor, ((1 * T + pos) * NB + b) * 512, [[NB * 512, 128], [1, 512]]), writes=[y1])
                    S.dma("sp", bg[:].rearrange("p a c -> p (a c)"), bass.AP(bgs.tensor, (b * T + pos) * 1024, [[1024, 128], [1, 1024]]), writes=[bg])
                    V("dve", LZ("tensor_tensor", out=yy[:], in0=y0[:], in1=y1[:], op=ALU.add), [y0, y1], [yy])
                    V("dve", LZ("tensor_reduce", out=s1[:], in_=yy[:], axis=AX.X, op=ALU.add), [yy], [s1])
                    V("dve", LZ("tensor_scalar", out=s1[:], in0=s1[:], scalar1=-1.0 / 64, scalar2=None, op0=ALU.mult), [s1], [s1])
                    V("dve", LZ("tensor_tensor", out=yy[:], in0=yy[:], in1=s1[:].unsqueeze(2).to_broadcast([128, 8, 64]), op=ALU.add), [yy, s1], [yy])
                    V("pool", LZ("tensor_tensor", out=ysq[:], in0=yy[:], in1=yy[:], op=ALU.mult), [yy], [ysq])
                    V("dve", LZ("tensor_reduce", out=s2[:], in_=ysq[:], axis=AX.X, op=ALU.add), [ysq], [s2])
                    V("act", LZ("activation", out=s2[:], in_=s2[:], func=AF.Sqrt, bias=epsl[:], scale=1.0 / 64), [s2, epsl], [s2])
                    V("dve", LZ("reciprocal", out=s2[:], in_=s2[:]), [s2], [s2])
                    V("dve", LZ("tensor_tensor", out=yy[:], in0=yy[:], in1=s2[:].unsqueeze(2).to_broadcast([128, 8, 64]), op=ALU.mult), [yy, s2], [yy])
                    yf = yy[:].rearrange("p h j -> p (h j)")
                    V("pool", LZ("tensor_tensor", out=yf, in0=yf, in1=lw[:], op=ALU.mult), [yy, lw], [yy])
                    V("pool", LZ("tensor_tensor", out=yf, in0=yf, in1=lb[:], op=ALU.add), [yy, lb], [yy])
                    V("dve", LZ("tensor_tensor", out=yf, in0=yf, in1=bg[:, 0, :], op=ALU.add), [yy, bg], [yy])
                    V("dve", LZ("tensor_tensor", out=rwb[:], in0=yf, in1=bg[:, 1, :], op=ALU.mult), [yy, bg], [rwb])
                    pb = bank(); pbv = pb[:].bitcast(BF16)
                    for cc in range(4):
                        V("pe", LZ("transpose", out=pbv[:, cc * 128:(cc + 1) * 128], in_=rwb[:, cc * 128:(cc + 1) * 128], identity=identb[:]), [rwb, identb], [pb])
                    V("act", LZ("copy", out=ct[:, :, tt * 128:(tt + 1) * 128], in_=pbv[:, 0:512].rearrange("p (a b) -> p a b", a=4, b=128)), [pb], [ct])
                S.dma("pool", bass.AP(cats.tensor, (b * G + g) * 128 * 2048 + 1024, [[2048, 128], [256, 4], [1, 256]]), ct[:], reads=[ct])
        phase_end()

        def load_w(Wt, src_t, l_off, rows_k, ncols, stg_):
            for k in range(rows_k):
                for c0 in range(0, ncols, 2048):
                    cw = min(2048, ncols - c0)
                    S.dma("sp", stg_[:, 0:cw], bass.AP(src_t.tensor, l_off + k * 128 * ncols + c0, [[ncols, 128], [1, cw]]), writes=[stg_])
                    if k % 2:
                        V("act", LZ("copy", out=Wt[:, k, c0:c0 + cw], in_=stg_[:, 0:cw]), [stg_], [Wt])
                    else:
                        V("dve", LZ("tensor_copy", out=Wt[:, k, c0:c0 + cw], in_=stg_[:, 0:cw]), [stg_], [Wt])
        Wo = alloc("Wo", [128, 8, 1024], BF16); W1 = alloc("W1", [128, 8, 4096], BF16); stg = alloc("stg", [128, 2048])
        load_w(Wo, w_out, l * D * D, 8, 1024, stg); load_w(W1, w_mlp1, l * D * DFF, 8, 4096, stg)
        cin = [alloc("cin", [128, 8, 256], BF16) for _ in range(2)]; xin = [alloc("xin", [128, 8, 256]) for _ in range(2)]
        x1 = alloc("x1", [128, 8, 256]); sq = alloc("sq", [128, 8, 256], BF16); rstd = alloc("rstd", [128, 256]); tmpa = alloc("tmpa", [128, 256])
        tn = alloc("tn", [128, 8, 256]); h2 = alloc("h2", [128, 8, 256], BF16); rl = [alloc("rl", [128, 256]) for _ in range(2)]; ut = alloc("ut", [128, 32, 256], BF16)
        for b in range(NB):
            for g in range(G):
                ci_ = cin[g % 2]; xa = xin[g % 2]; row = NB if g == 0 else b
                S.dma("sp", ci_[:], cats[b, g].rearrange("p (a b) -> p a b", a=8, b=256), writes=[ci_])
                S.dma("sp", xa[:], xT[b, g].rearrange("p (a b) -> p a b", a=8, b=256), writes=[xa])
                for m in range(8):
                    pb = bank()
                    for k in range(8):
                        V("pe", LZ("matmul", pb[:, 0:256], lhsT=Wo[:, k, m * 128:(m + 1) * 128], rhs=ci_[:, k, :], start=(k == 0), stop=(k == 7)), [Wo, ci_], [pb])
                    V("dve", LZ("scalar_tensor_tensor", out=x1[:, m, :], in0=pb[:, 0:256], scalar=MODS[l][:, 16 + m, row:row + 1], in1=xa[:, m, :], op0=ALU.mult, op1=ALU.add), [pb, MODS[l], xa], [x1])
                S.dma("pool", xT[b, g].rearrange("p (a b) -> p a b", a=8, b=256), x1[:], reads=[x1])
                rms_fm((x1, x1[:]), 8, 256, 1.0 / D, sq, rstd, tmpa)
                V("dve", LZ("tensor_tensor", out=tn[:], in0=x1[:], in1=rstd[:].unsqueeze(1).to_broadcast([128, 8, 256]), op=ALU.mult), [x1, rstd], [tn])
                for k in range(8):
                    if k % 2:
                        V("act", LZ("activation", out=h2[:, k, :], in_=tn[:, k, :], func=AF.Identity, bias=MODS[l][:, 24 + k, row:row + 1], scale=COEF2[l][:, k, row:row + 1]), [tn, MODS[l], COEF2[l]], [h2])
                    else:
                        V("pool", LZ("tensor_scalar", out=h2[:, k, :], in0=tn[:, k, :], scalar1=COEF2[l][:, k, row:row + 1], scalar2=MODS[l][:, 24 + k, row:row + 1], op0=ALU.mult, op1=ALU.add), [tn, MODS[l], COEF2[l]], [h2])
                for m in range(32):
                    pb = bank(); r_ = rl[m % 2]
                    for k in range(8):
                        V("pe", LZ("matmul", pb[:, 0:256], lhsT=W1[:, k, m * 128:(m + 1) * 128], rhs=h2[:, k, :], start=(k == 0), stop=(k == 7)), [W1, h2], [pb])
                    V("act", LZ("activation", out=r_[:], in_=pb[:, 0:256], func=AF.Relu), [pb], [r_])
                    V("pool" if m % 2 else "dve", LZ("tensor_tensor", out=ut[:, m, :], in0=r_[:], in1=r_[:], op=ALU.mult), [r_], [ut])
                S.dma("pool", us[b, g].rearrange("p (a b) -> p a b", a=32, b=256), ut[:], reads=[ut])
        phase_end()

        W2 = alloc("W2", [128, 32, 1024], BF16); stg = alloc("stg", [128, 2048])
        load_w(W2, w_mlp2, l * DFF * D, 32, 1024, stg)
        uin = [alloc("uin", [128, 32, 256], BF16) for _ in range(2)]; xin = [alloc("xin", [128, 8, 256]) for _ in range(2)]; x2 = [alloc("x2", [128, 8, 256]) for _ in range(2)]
        for b in range(NB):
            for g in range(G):
                u_ = uin[g % 2]; xa = xin[g % 2]; xo = x2[g % 2]; row = NB if g == 0 else b
                S.dma("sp", u_[:], us[b, g].rearrange("p (a b) -> p a b", a=32, b=256), writes=[u_])
                S.dma("sp", xa[:], xT[b, g].rearrange("p (a b) -> p a b", a=8, b=256), writes=[xa])
                for m in range(8):
                    pb = bank()
                    for k in range(32):
                        V("pe", LZ("matmul", pb[:, 0:256], lhsT=W2[:, k, m * 128:(m + 1) * 128], rhs=u_[:, k, :], start=(k == 0), stop=(k == 31)), [W2, u_], [pb])
                    V("dve", LZ("scalar_tensor_tensor", out=xo[:, m, :], in0=pb[:, 0:256], scalar=MODS[l][:, 40 + m, row:row + 1], in1=xa[:, m, :], op0=ALU.mult, op1=ALU.add), [pb, MODS[l], xa], [xo])
                S.dma("pool", xT[b, g].rearrange("p (a b) -> p a b", a=8, b=256), xo[:], reads=[xo])
        phase_end()

    if last:
        fg = alloc("fg", [128, 8]); fm_vec(fg, fg[:], final_g, 0, 8)
        xin = [alloc("xin", [128, 8, 256]) for _ in range(2)]; sq = alloc("sq", [128, 8, 256], BF16); rstd = alloc("rstd", [128, 256]); tmpa = alloc("tmpa", [128, 256])
        tn = alloc("tn", [128, 8, 256]); yo = [alloc("yo", [128, 1024]) for _ in range(2)]
        io = 0
        for b in range(NB):
            for g in range(1, G):
                xa = xin[g % 2]
                S.dma("sp", xa[:], xT[b, g].rearrange("p (a b) -> p a b", a=8, b=256), writes=[xa])
                rms_fm((xa, xa[:]), 8, 256, 1.0 / D, sq, rstd, tmpa)
                V("dve", LZ("tensor_tensor", out=tn[:], in0=xa[:], in1=rstd[:].unsqueeze(1).to_broadcast([128, 8, 256]), op=ALU.mult), [xa, rstd], [tn])
                V("pool", LZ("tensor_tensor", out=tn[:], in0=tn[:], in1=fg[:].unsqueeze(2).to_broadcast([128, 8, 256]), op=ALU.mult), [tn, fg], [tn])
                for tt in range(2):
                    yt = yo[io % 2]; io += 1
                    for hf in range(2):
                        pb = bank()
                        for j in range(4):
                            V("pe", LZ("transpose", out=pb[:, j * 128:(j + 1) * 128], in_=tn[:, hf * 4 + j, tt * 128:(tt + 1) * 128], identity=ident[:]), [tn, ident], [pb])
                        if hf:
                            V("act", LZ("copy", out=yt[:, hf * 512:(hf + 1) * 512], in_=pb[:, 0:512]), [pb], [yt])
                        else:
                            V("dve", LZ("tensor_copy", out=yt[:, hf * 512:(hf + 1) * 512], in_=pb[:, 0:512]), [pb], [yt])
                    r0 = (g - 1) * 256 + tt * 128
                    out_dmas.append(S.dma("sp", y_out[b, r0:r0 + 128, :], yt[:], reads=[yt]))
    stats = S.emit(final_waits=out_dmas)
    S.es.close()
    return nc, stats


def _tables():
    pos = np.arange(SEQ)
    row = (pos // 64).astype(np.float32); col = (pos % 64).astype(np.float32)
    inv = (10000.0 ** (-np.arange(8, dtype=np.float32) / 8)).astype(np.float32)
    ar = row[:, None] * inv; ac = col[:, None] * inv
    ang = np.concatenate([ar, ar, ac, ac], axis=-1).astype(np.float32)
    cos_t = np.ones((96, T), np.float32); sin_t = np.zeros((96, T), np.float32)
    cos_t[64:96, TC:] = np.cos(ang).T; sin_t[64:96, TC:] = np.sin(ang).T
    return cos_t, sin_t


_CACHE = {}
_LAYER_W = ["ada_w", "ada_b", "norm1_g", "norm2_g", "w_in", "q_norm_g", "w_uq", "kv_norm_g", "w_ukv", "shift_mu", "decay_w0", "decay_w2",
            "iclr_a0", "iclr_a2", "gate_g2", "k_k", "k_a", "r_k", "lnx_w", "lnx_b", "w_out", "w_mlp1", "w_mlp2"]
_VRES_W = ["vres_v1", "vres_v2", "vres_v0"]


def _f(a):
    return np.ascontiguousarray(np.asarray(a, dtype=np.float32))


def run(inputs, NB, L, ncores):
    key = (NB, L)
    if key not in _CACHE:
        _CACHE[key] = build(NB, L)
    nc, stats = _CACHE[key]
    cos_t, sin_t = _tables()
    shared = {k: _f(v) for k, v in inputs.items() if k not in ("x", "c", "ctx", "c_ctx", "final_g", "r_k")}
    shared["c_ctx"] = _f(inputs["c_ctx"]).reshape(1, D); shared["final_g"] = _f(inputs["final_g"]).reshape(1, D)
    shared["r_k"] = _f(inputs["r_k"]).reshape(4, R)
    shared["ident"] = np.eye(128, dtype=np.float32); shared["cos_t"] = cos_t; shared["sin_t"] = sin_t
    in_maps = []
    for i in range(ncores):
        m = dict(shared)
        m["x"] = _f(inputs["x"][i * NB:(i + 1) * NB]); m["c"] = _f(inputs["c"][i * NB:(i + 1) * NB]); m["ctx"] = _f(inputs["ctx"][i * NB:(i + 1) * NB])
        in_maps.append(m)
    res = run_bass_kernel_spmd(nc, in_maps, core_ids=list(range(ncores)))
    return np.concatenate([r["y"] for r in res.results], axis=0)


def run_layers(inputs, NB, L, ncores):
    cos_t, sin_t = _tables()
    state = [None] * ncores
    out = None
    for l in range(L):
        first = l == 0; last = l == L - 1; has_vres = l > 0
        key = ("layer", NB, has_vres, first, last)
        if key not in _CACHE:
            _CACHE[key] = build(NB, 1, mode=(has_vres, first, last))
        nc, stats = _CACHE[key]
        shared = {}
        for k in _LAYER_W:
            a = _f(inputs[k])
            if k == "r_k":
                a = a.reshape(a.shape[0], R)
            shared[k] = np.ascontiguousarray(a[l:l + 1])
        vl = max(l - 1, 0)
        for k in _VRES_W:
            shared[k] = np.ascontiguousarray(_f(inputs[k])[vl:vl + 1])
        shared["c_ctx"] = _f(inputs["c_ctx"]).reshape(1, D); shared["final_g"] = _f(inputs["final_g"]).reshape(1, D)
        shared["ident"] = np.eye(128, dtype=np.float32); shared["cos_t"] = cos_t; shared["sin_t"] = sin_t
        in_maps = []
        for i in range(ncores):
            m = dict(shared)
            m["c"] = _f(inputs["c"][i * NB:(i + 1) * NB])
            if first:
                m["x"] = _f(inputs["x"][i * NB:(i + 1) * NB]); m["ctx"] = _f(inputs["ctx"][i * NB:(i + 1) * NB])
            else:
                m["xT_i"] = state[i][0]; m["vfs_i"] = state[i][1]
            in_maps.append(m)
        res = run_bass_kernel_spmd(nc, in_maps, core_ids=list(range(ncores)))
        if last:
            out = np.concatenate([r["y"] for r in res.results], axis=0)
        else:
            state = [(r["xT_o"], r["vfs_o"]) for r in res.results]
    return out


def kernel(**inputs):
    return run(inputs, 4, 4, 8).astype(np.float32)
```
